# Optimizing a Trainium2 kernel written in Bass

```python
import math
import jax, jax.numpy as jnp
from jax import lax
import numpy as np

D_MODEL = 1024
BATCH = 8
SEQ = 4096
DEPTH = 1

MIX_WIDTH = D_MODEL
ATTN_WIDTH = MIX_WIDTH // 2
POOL_WIDTH = MIX_WIDTH - ATTN_WIDTH
HEAD_DIM = 64
N_HEADS = ATTN_WIDTH // HEAD_DIM
POOL_WINDOWS = (2, 4, 8, 16)
N_POOL_GROUPS = len(POOL_WINDOWS)
POOL_GROUP = POOL_WIDTH // N_POOL_GROUPS
D_FF = 4 * D_MODEL
PLE_DIM = 256
Q_BLOCK = 128
LN_EPS = 1e-5
RMS_EPS = 1e-6
DN_ALPHA = float((2 * DEPTH) ** 0.25)
DN_BETA = float((8 * DEPTH) ** -0.25)
PROJ_WIDTH = 3 * ATTN_WIDTH + POOL_WIDTH

kernel_name = "hymba_stickbreak_pool_deepnorm"


def layer_norm(x, g, b):
    xf = x.astype(jnp.float32)
    mu = jnp.mean(xf, axis=-1, keepdims=True)
    var = jnp.mean(jnp.square(xf - mu), axis=-1, keepdims=True)
    y = (xf - mu) * lax.rsqrt(var + LN_EPS)
    return (y * g.astype(jnp.float32) + b.astype(jnp.float32)).astype(x.dtype)


def stick_breaking_attention(q, k, v):
    B, H, S, Dh = q.shape
    n_blk = S // Q_BLOCK
    scale = 1.0 / math.sqrt(Dh)
    q_blocks = q.reshape(B, H, n_blk, Q_BLOCK, Dh).transpose(2, 0, 1, 3, 4)
    starts = jnp.arange(n_blk, dtype=jnp.int32) * Q_BLOCK
    k_pos = jnp.arange(S, dtype=jnp.int32)
    kf = k.astype(jnp.float32)
    vf = v.astype(jnp.float32)

    def one_block(args):
        q_blk, start = args
        z = jnp.einsum("bhqd,bhkd->bhqk", q_blk.astype(jnp.float32), kf) * scale
        q_pos = start + jnp.arange(Q_BLOCK, dtype=jnp.int32)
        mask = k_pos[None, :] < q_pos[:, None]
        log_not = jnp.where(mask, jax.nn.log_sigmoid(-z), 0.0)
        suffix = lax.cumsum(log_not, axis=3, reverse=True) - log_not
        weights = jnp.where(mask, jnp.exp(jax.nn.log_sigmoid(z) + suffix), 0.0)
        return jnp.einsum("bhqk,bhkd->bhqd", weights, vf)

    out = lax.map(one_block, (q_blocks, starts))
    return out.transpose(1, 2, 0, 3, 4).reshape(B, H, S, Dh)


def multiscale_pool(u, w_pool, pool_scale):
    B, S, _ = u.shape
    uf = u.astype(jnp.float32)
    csum = jnp.cumsum(uf, axis=1)
    pos = jnp.arange(S, dtype=jnp.int32)
    diffs = []
    for g, w in enumerate(POOL_WINDOWS):
        sl = slice(g * POOL_GROUP, (g + 1) * POOL_GROUP)
        cg = csum[..., sl]
        lag = jnp.pad(cg, ((0, 0), (w, 0), (0, 0)))[:, :S]
        count = jnp.minimum(pos + 1, w).astype(jnp.float32)[None, :, None]
        diffs.append((cg - lag) / count - uf[..., sl])
    d = jnp.stack(diffs, axis=2)
    y = jnp.einsum("bsgc,gcd->bsgd", d, w_pool.astype(jnp.float32))
    return y.reshape(B, S, POOL_WIDTH) * pool_scale.astype(jnp.float32)


def setup_inputs(seed: int = 0) -> dict:
    key = jax.random.key(seed)
    ks = jax.random.split(key, 24)
    nrm = lambda k, shape, s: jax.random.normal(k, shape, jnp.float32) * s
    L = DEPTH
    x = jax.random.normal(ks[0], (BATCH, SEQ, D_MODEL), jnp.float32)
    p = jax.random.normal(ks[1], (L, BATCH, SEQ, PLE_DIM), jnp.float32)
    emb_ln_g = 1.0 + nrm(ks[2], (D_MODEL,), 0.02)
    emb_ln_b = nrm(ks[3], (D_MODEL,), 0.02)
    col_scale = jnp.concatenate([
        jnp.ones((2 * ATTN_WIDTH,), jnp.float32),
        jnp.full((ATTN_WIDTH + POOL_WIDTH,), DN_BETA, jnp.float32)])
    w_in = nrm(ks[4], (L, D_MODEL, PROJ_WIDTH), D_MODEL ** -0.5) * col_scale
    attn_out_g = 1.0 + nrm(ks[5], (L, ATTN_WIDTH), 0.02)
    w_pool = nrm(ks[6], (L, N_POOL_GROUPS, POOL_GROUP, POOL_GROUP), POOL_GROUP ** -0.5 * DN_BETA)
    pool_scale = 1.0 + nrm(ks[7], (L, POOL_WIDTH), 0.02)
    w_out = nrm(ks[8], (L, MIX_WIDTH, D_MODEL), MIX_WIDTH ** -0.5 * DN_BETA)
    ln1_g = 1.0 + nrm(ks[9], (L, D_MODEL), 0.02)
    ln1_b = nrm(ks[10], (L, D_MODEL), 0.02)
    w_up = nrm(ks[11], (L, D_MODEL, D_FF), D_MODEL ** -0.5 * DN_BETA)
    w_down = nrm(ks[12], (L, D_FF, D_MODEL), D_FF ** -0.5 * DN_BETA)
    ln2_g = 1.0 + nrm(ks[13], (L, D_MODEL), 0.02)
    ln2_b = nrm(ks[14], (L, D_MODEL), 0.02)
    w_ple = nrm(ks[15], (L, PLE_DIM, D_MODEL), PLE_DIM ** -0.5 * DN_BETA)
    w_ple_gate = nrm(ks[16], (L, D_MODEL, D_MODEL), D_MODEL ** -0.5)
    ln3_g = 1.0 + nrm(ks[17], (L, D_MODEL), 0.02)
    ln3_b = nrm(ks[18], (L, D_MODEL), 0.02)
    return {"x": x, "p": p, "emb_ln_g": emb_ln_g, "emb_ln_b": emb_ln_b,
            "w_in": w_in, "attn_out_g": attn_out_g, "w_pool": w_pool, "pool_scale": pool_scale,
            "w_out": w_out, "ln1_g": ln1_g, "ln1_b": ln1_b, "w_up": w_up, "w_down": w_down,
            "ln2_g": ln2_g, "ln2_b": ln2_b, "w_ple": w_ple, "w_ple_gate": w_ple_gate,
            "ln3_g": ln3_g, "ln3_b": ln3_b}


def reference(x, p, emb_ln_g, emb_ln_b, w_in, attn_out_g, w_pool, pool_scale, w_out,
              ln1_g, ln1_b, w_up, w_down, ln2_g, ln2_b, w_ple, w_ple_gate, ln3_g, ln3_b):
    B, S, D = x.shape
    dt = x.dtype
    x = layer_norm(x, emb_ln_g, emb_ln_b)
    for i in range(DEPTH):
        proj = x @ w_in[i]
        q, k, v, u = jnp.split(proj, [ATTN_WIDTH, 2 * ATTN_WIDTH, 3 * ATTN_WIDTH], axis=-1)
        to_heads = lambda t: t.reshape(B, S, N_HEADS, HEAD_DIM).transpose(0, 2, 1, 3)
        o = stick_breaking_attention(to_heads(q), to_heads(k), to_heads(v))
        o = o * lax.rsqrt(jnp.mean(o * o, axis=-1, keepdims=True) + RMS_EPS)
        o = o.transpose(0, 2, 1, 3).reshape(B, S, ATTN_WIDTH) * attn_out_g[i].astype(jnp.float32)
        pooled = multiscale_pool(u, w_pool[i], pool_scale[i])
        mixed = jnp.concatenate([o, pooled], axis=-1).astype(dt) @ w_out[i]
        x = layer_norm(DN_ALPHA * x + mixed, ln1_g[i], ln1_b[i])
        h = jnp.square(jax.nn.relu(x @ w_up[i])) @ w_down[i]
        x = layer_norm(DN_ALPHA * x + h, ln2_g[i], ln2_b[i])
        ple = (p[i] @ w_ple[i]) * jax.nn.sigmoid(x @ w_ple_gate[i])
        x = layer_norm(DN_ALPHA * x + ple, ln3_g[i], ln3_b[i])
    return x
```

```python
import contextlib
import itertools
import numpy as np
import ml_dtypes
import concourse.bass as bass
import concourse.mybir as mybir
from concourse.bass_utils import run_bass_kernel_spmd

F32 = mybir.dt.float32
BF16 = mybir.dt.bfloat16
AF = mybir.ActivationFunctionType
ALU = mybir.AluOpType

D = 1024
DH = 64
DFF = 4096
PLE = 256
TB = 512
TT = 128
ALPHA = float(2.0 ** 0.25)
LN_EPS = 1e-5
RMS_EPS = 1e-6
NEG = -30000.0
POOL_WINDOWS = (2, 4, 8, 16)
NWSLOT = 3
NPIECE = 25

C_IDENT, C_TRI, C_ONES, C_BLK = 0, 1, 2, 3
C_POOL = 4
C_MASK = 16
NCB = 17


class Sched:
    def __init__(self, nc):
        self.nc = nc
        self.ops = []
        self.last_w = {}
        self.readers = {}
        self.eng_sem = {e: nc.alloc_semaphore("sem_" + e) for e in ("pe", "act", "dve", "pool")}
        self.dma_sems = {}

    def add(self, eng, fn, reads=(), writes=(), dma_key=None):
        oid = len(self.ops)
        deps = set()
        for r in reads:
            if r in self.last_w:
                deps.add(self.last_w[r])
        for w in writes:
            if w in self.last_w:
                deps.add(self.last_w[w])
            for rd in self.readers.get(w, ()):
                deps.add(rd)
        op = dict(id=oid, eng=eng, fn=fn, deps=deps, dma_key=dma_key, need_inc=False, cnt=0)
        if dma_key is not None:
            if dma_key not in self.dma_sems:
                self.dma_sems[dma_key] = [self.nc.alloc_semaphore("dsem%d" % len(self.dma_sems)), 0]
            ent = self.dma_sems[dma_key]
            ent[1] += 16
            op["dma_sem"] = ent[0]
            op["dma_val"] = ent[1]
        self.ops.append(op)
        for w in writes:
            self.last_w[w] = oid
            self.readers[w] = []
        for r in reads:
            self.readers.setdefault(r, []).append(oid)
        return oid

    def finalize(self):
        ops = self.ops
        for op in ops:
            for d in op["deps"]:
                B = ops[d]
                if B["dma_key"] is not None:
                    continue
                if B["eng"] == "pe" and op["eng"] == "pe" and op["dma_key"] is None:
                    continue
                B["need_inc"] = True
        cnt = {e: 0 for e in self.eng_sem}
        for op in ops:
            if op["dma_key"] is None and op["need_inc"]:
                cnt[op["eng"]] += 1
                op["cnt"] = cnt[op["eng"]]
        self.by_eng = {}
        for op in ops:
            self.by_eng.setdefault(op["eng"], []).append(op)

    def emit_engine(self, ename, eng, final_waits=()):
        ops = self.ops
        waited = {}
        for op in self.by_eng.get(ename, []):
            need = {}
            for d in op["deps"]:
                B = ops[d]
                if B["dma_key"] is not None:
                    sem, val = B["dma_sem"], B["dma_val"]
                else:
                    if B["eng"] == "pe" and ename == "pe" and op["dma_key"] is None:
                        continue
                    sem, val = self.eng_sem[B["eng"]], B["cnt"]
                k = sem.num
                if k not in need or need[k][1] < val:
                    need[k] = (sem, val)
            for k, (sem, val) in need.items():
                if waited.get(k, 0) >= val:
                    continue
                waited[k] = val
                eng.wait_ge(sem, val)
            ins = op["fn"](eng)
            if op["dma_key"] is not None:
                ins.then_inc(op["dma_sem"], 16)
            elif op["need_inc"]:
                ins.then_inc(self.eng_sem[ename], 1)
        for sem, val in final_waits:
            eng.wait_ge(sem, val)


def build_nc(S, debug=False):
    NBLK = S // TB
    NT = S // TT
    nc = bass.Bass("TRN2", target_bir_lowering=False)

    def din(name, shape, dt=F32):
        return nc.dram_tensor(name, shape, dt, kind="ExternalInput").ap()

    x_d = din("x", [S, D])
    p_d = din("p", [S, PLE])
    w_in_d = din("w_in", [D, 2048])
    w_out_d = din("w_out", [D, D])
    w_up_d = din("w_up", [D, DFF])
    w_down_d = din("w_down", [DFF, D])
    w_ple_d = din("w_ple", [PLE, D])
    w_gate_d = din("w_gate", [D, D])
    w_pool_d = din("w_pool", [4, 128, 128])
    lnp_d = din("lnp", [8, D])
    cols_d = din("cols", [128, 8])
    cbf_d = din("cbf", [128, NCB, 128], BF16)
    identf_d = din("identf", [128, 128])
    out_d = nc.dram_tensor("out", [S, D], F32, kind="ExternalOutput").ap()
    wsc_d = nc.dram_tensor("wsc", [NPIECE, 128, 4096], BF16, kind="Internal").ap()
    if debug:
        dbg_x0 = nc.dram_tensor("dbg_x0", [S, D], F32, kind="ExternalOutput").ap()
        dbg_x1 = nc.dram_tensor("dbg_x1", [S, D], F32, kind="ExternalOutput").ap()
        dbg_x2 = nc.dram_tensor("dbg_x2", [S, D], F32, kind="ExternalOutput").ap()

    es = contextlib.ExitStack()
    with es:
        def sb(name, shape, dt):
            return es.enter_context(nc.sbuf_tensor(name, shape, dt))

        KT = sb("KT", [128, 4, S], BF16)
        V = sb("V", [128, NT, 512], BF16)
        U = sb("U", [128, 5, 512], BF16)
        wsl = sb("wsl", [128, NWSLOT, 4096], BF16)
        gb = sb("gb", [128, 2, D], F32)
        xres = sb("xres", [128, 2, 4, D], F32)
        actT = sb("actT", [128, 8, 512], BF16)
        qT = sb("qT", [128, 2, 4, 512], BF16)
        mixA = sb("mixA", [128, 4, 512], BF16)
        mixP = sb("mixP", [128, 2, 4, 512], BF16)
        e1 = sb("e1", [128, 2, 512], F32)
        sp_r = sb("sp_r", [128, 3, 2, 512], BF16)
        R_r = sb("R_r", [128, 3, 2, 512], BF16)
        W_r = sb("W_r", [128, 2, 2, 512], BF16)
        hT = sb("hT", [128, 16, 512], BF16)
        r32 = sb("r32", [128, 1, 512], F32)
        rst = r32
        dT = sb("dT", [128, 1, 512], BF16)
        sqb = dT[:, 0, :]
        cbf = sb("cbf_s", [128, NCB, 128], BF16)
        identf = sb("identf_s", [128, 128], F32)
        wpool = sb("wpool_s", [128, 4, 128], BF16)
        cols = sb("cols_s", [128, 8], F32)
        st = sb("st", [128, 2, 4, 12], F32)
        mv = sb("mv", [128, 2, 4, 2], F32)
        sm = sb("sm", [128, 2, 4, 4], F32)
        ps = es.enter_context(nc.psum_tensor("ps", [128, 8, 512], F32))

        hflat = hT[:].rearrange("p a b -> p (a b)")
        pbuf = hflat[:, 0:2048].bitcast(F32).rearrange("p (a b) -> p a b", a=4)
        pT = hflat[:, 2048:3072].rearrange("p (a b) -> p a b", a=2)
        egb = hflat[:, 3072:4096].bitcast(F32)
        tmpb = hflat[:, 4096:5120].bitcast(F32)
        HT = lambda s_: ("hT", s_)
        PBUF_R = [HT(0), HT(1), HT(2), HT(3)]
        PT_R = [HT(4), HT(5)]
        EG_R = [HT(6), HT(7)]
        TMP_R = [HT(8), HT(9)]

        sch = Sched(nc)
        add = sch.add

        def bank(k):
            return ("bank", k)

        add("sp", lambda e: e.dma_start(out=cbf[:], in_=cbf_d), writes=[("cbf",)], dma_key="cbf")
        add("sp", lambda e: e.dma_start(out=identf[:], in_=identf_d), writes=[("identf",)], dma_key="identf")
        add("sp", lambda e: e.dma_start(out=cols[:], in_=cols_d), writes=[("cols",)], dma_key="cols")
        add("pool", lambda e: e.dma_start(out=wpool[:], in_=w_pool_d.rearrange("g c d -> c g d")),
            writes=[("wpool",)], dma_key="wpool")
        add("pool", lambda e: e.memset(U[:, 4, :], 0.0), writes=[("U", 4)])

        def piece_src(n):
            if n < 4:
                return w_in_d.rearrange("(kc p) f -> p kc f", p=128)[:, :, n * 512:(n + 1) * 512], 8
            if n < 6:
                h = n - 4
                return w_out_d.rearrange("(kc p) f -> p kc f", p=128)[:, :, h * 512:(h + 1) * 512], 8
            if n < 14:
                j = n - 6
                return w_up_d.rearrange("(kc p) f -> p kc f", p=128)[:, :, j * 512:(j + 1) * 512], 8
            if n < 22:
                m = n - 14
                fh, oh, q = m // 4, (m // 2) % 2, m % 2
                c0 = fh * 16 + q * 8
                return w_down_d.rearrange("(c p) o -> p c o", p=128)[:, c0:c0 + 8, oh * 512:(oh + 1) * 512], 8
            if n < 24:
                h = n - 22
                return w_gate_d.rearrange("(kc p) f -> p kc f", p=128)[:, :, h * 512:(h + 1) * 512], 8
            return w_ple_d.rearrange("(c p) o -> p c o", p=128), 2

        for n in range(NPIECE):
            src, a = piece_src(n)
            if n < 24:
                dst = wsc_d[n].rearrange("p (a b) -> p a b", a=a)
            else:
                dst = wsc_d[n][:, 0:2048].rearrange("p (a b) -> p a b", a=2)
            add("pool", (lambda e, dst=dst, src=src: e.dma_start(out=dst, in_=src)),
                writes=[("wsc", n)], dma_key=("wsc", n))

        PRE_SEQ = [0, 1, 2, 3]
        POST_SEQ = [4, 5]
        for fh in range(2):
            POST_SEQ += [6 + fh * 4 + j for j in range(4)]
            for oh in range(2):
                for tp in range(2):
                    POST_SEQ += [14 + fh * 4 + oh * 2, 14 + fh * 4 + oh * 2 + 1]
        POST_SEQ += [22, 23, 24]
        full_seq = list(PRE_SEQ)
        for b in range(NBLK):
            if b >= 1:
                full_seq += POST_SEQ
            if b + 1 < NBLK:
                full_seq += PRE_SEQ
        full_seq += POST_SEQ
        wstate = {"issued": 0, "ptr": 0}

        def issue_upto(g):
            while wstate["issued"] <= min(g, len(full_seq) - 1):
                gi = wstate["issued"]
                n = full_seq[gi]
                s = gi % NWSLOT
                if n < 24:
                    o_ap, i_ap = wsl[:, s, :], wsc_d[n]
                else:
                    o_ap, i_ap = wsl[:, s, 0:2048], wsc_d[n][:, 0:2048]
                add("sp", (lambda e, o_ap=o_ap, i_ap=i_ap: e.dma_start(out=o_ap, in_=i_ap)),
                    reads=[("wsc", n)], writes=[("wslot", s)], dma_key=("wslot", s))
                wstate["issued"] += 1

        def use_pieces(expect):
            k = len(expect)
            assert k <= NWSLOT
            g = wstate["ptr"]
            assert full_seq[g:g + k] == list(expect), (full_seq[g:g + k], expect)
            wstate["ptr"] += k
            issue_upto(g + NWSLOT - 1)
            return [(g + j) % NWSLOT for j in range(k)]

        def wview(s, a):
            return wsl[:, s, :].rearrange("p (a b) -> p a b", a=a)

        sbc = [0]

        def next_bank():
            sbc[0] += 1
            return 6 + sbc[0] % 2

        def load_gb(k):
            for j in range(2):
                add("sp", (lambda e, j=j, k=k: e.dma_start(
                    out=gb[:, j, :], in_=lnp_d[2 * k + j:2 * k + j + 1, :].partition_broadcast(128))),
                    writes=[("gb", j)], dma_key=("gb", j))

        def ln_stats(par, i):
            xr = ("xres", par, i)
            add("dve", lambda e: e.bn_stats(out=st[:, par, i, 0:6], in_=xres[:, par, i, 0:512]),
                reads=[xr], writes=[("st", par, i, 0)])
            add("dve", lambda e: e.bn_stats(out=st[:, par, i, 6:12], in_=xres[:, par, i, 512:1024]),
                reads=[xr], writes=[("st", par, i, 1)])
            add("dve", lambda e: e.bn_aggr(out=mv[:, par, i, :], in_=st[:, par, i, :]),
                reads=[("st", par, i, 0), ("st", par, i, 1)], writes=[("mv", par, i)])

        def transpose_tile(par, i):
            for c in range(8):
                add("pe", (lambda e, c=c: e.transpose(
                    out=ps[:, 6 + c // 4, (c % 4) * 128:(c % 4 + 1) * 128], in_=xres[:, par, i, c * 128:(c + 1) * 128],
                    identity=identf[:])),
                    reads=[("xres", par, i), ("identf",)], writes=[bank(6 + c // 4)])
            for h in range(2):
                add("dve", (lambda e, h=h: e.tensor_copy(
                    out=actT[:, 4 * h:4 * h + 4, i * 128:(i + 1) * 128],
                    in_=ps[:, 6 + h, :].rearrange("p (a b) -> p a b", a=4))),
                    reads=[bank(6 + h)], writes=[("actT", c) for c in range(4 * h, 4 * h + 4)])

        def ln_finish(par, after=None, do_T=True):
            yield ("wait", 1)
            for i in range(4):
                add("act", lambda e, i=i: e.activation(out=sm[:, par, i, 0:1], in_=mv[:, par, i, 1:2], func=AF.Ln,
                                                       bias=LN_EPS, scale=1.0),
                    reads=[("mv", par, i)], writes=[("sm", par, i, 0)])
                add("act", lambda e, i=i: e.activation(out=sm[:, par, i, 1:2], in_=sm[:, par, i, 0:1], func=AF.Exp,
                                                       scale=-0.5),
                    reads=[("sm", par, i, 0)], writes=[("sm", par, i, 1)])
            yield 0.6
            yield ("wait", 1)
            for i in range(4):
                xr = ("xres", par, i)
                xt = xres[:, par, i, :]
                add("dve", lambda e, i=i: e.tensor_scalar(out=sm[:, par, i, 2:3], in0=mv[:, par, i, 0:1],
                                                          scalar1=sm[:, par, i, 1:2], scalar2=-1.0,
                                                          op0=ALU.mult, op1=ALU.mult),
                    reads=[("mv", par, i), ("sm", par, i, 1)], writes=[("sm", par, i, 2)])
                add("dve", lambda e, i=i, xt=xt: e.tensor_scalar(out=xt, in0=xt, scalar1=sm[:, par, i, 1:2],
                                                                 scalar2=sm[:, par, i, 2:3], op0=ALU.mult, op1=ALU.add),
                    reads=[xr, ("sm", par, i, 1), ("sm", par, i, 2)], writes=[xr])
                yield 0.8
                add("dve", lambda e, xt=xt: e.tensor_tensor(out=xt, in0=xt, in1=gb[:, 0, :], op=ALU.mult),
                    reads=[xr, ("gb", 0)], writes=[xr])
                yield 1.2
                add("dve", lambda e, xt=xt: e.tensor_tensor(out=xt, in0=xt, in1=gb[:, 1, :], op=ALU.add),
                    reads=[xr, ("gb", 1)], writes=[xr])
                if after is not None:
                    after(i)
                yield 1.2
                if do_T and i >= 1:
                    transpose_tile(par, i - 1)
                    yield 2.5
            if do_T:
                yield ("wait", 1)
                transpose_tile(par, 3)
                yield 2.5

        def dump(dst, par, i, t0, key):
            add("sp", (lambda e: e.dma_start(out=dst[t0 + i * 128:t0 + (i + 1) * 128, :], in_=xres[:, par, i, :])),
                reads=[("xres", par, i)], dma_key=(key, par, i))

        def to_actT(par):
            prev = None
            for c in range(8):
                bk = next_bank()
                for i in range(4):
                    add("pe", (lambda e, c=c, i=i, bk=bk: e.transpose(
                        out=ps[:, bk, i * 128:(i + 1) * 128], in_=xres[:, par, i, c * 128:(c + 1) * 128],
                        identity=identf[:])),
                        reads=[("xres", par, i), ("identf",)], writes=[bank(bk)])
                if prev is not None:
                    pc, pbk = prev
                    add("dve", (lambda e, pc=pc, pbk=pbk: e.tensor_copy(out=actT[:, pc, :], in_=ps[:, pbk, :])),
                        reads=[bank(pbk)], writes=[("actT", pc)])
                prev = (c, bk)
                yield 1.8
            pc, pbk = prev
            add("dve", (lambda e: e.tensor_copy(out=actT[:, pc, :], in_=ps[:, pbk, :])),
                reads=[bank(pbk)], writes=[("actT", pc)])

        def pre(b):
            par = b % 2
            t0 = b * TB
            load_gb(0)
            for i in range(4):
                add("sp", (lambda e, i=i: e.dma_start(out=xres[:, par, i, :], in_=x_d[t0 + i * 128:t0 + (i + 1) * 128, :])),
                    writes=[("xres", par, i)], dma_key=("xres", par, i))
            for i in range(4):
                ln_stats(par, i)
                yield 1.3
            yield from ln_finish(par, (lambda i: dump(dbg_x0, par, i, t0, "dbg0")) if debug else None)
            for which in range(2):
                s, = use_pieces([which])
                wv = wview(s, 8)
                prev = None
                for hp in range(5):
                    if hp < 4:
                        bk = next_bank()
                        for kc in range(8):
                            add("pe", (lambda e, hp=hp, kc=kc, bk=bk, wv=wv: e.matmul(
                                ps[:, bk, :], lhsT=wv[:, kc, hp * 128:(hp + 1) * 128], rhs=actT[:, kc, :],
                                start=(kc == 0), stop=(kc == 7))),
                                reads=[("wslot", s), ("actT", kc)], writes=[bank(bk)])
                    if prev is not None:
                        php, pbk = prev
                        if which == 0:
                            add("dve", (lambda e, php=php, pbk=pbk: e.tensor_scalar(
                                out=qT[:, par, php, :], in0=ps[:, pbk, :], scalar1=0.125, scalar2=None, op0=ALU.mult)),
                                reads=[bank(pbk)], writes=[("qT", par, php)])
                        else:
                            add("dve", (lambda e, php=php, pbk=pbk: e.tensor_copy(
                                out=KT[:, php, t0:t0 + TB], in_=ps[:, pbk, :])),
                                reads=[bank(pbk)], writes=[("KT", b, php)])
                    prev = (hp, bk) if hp < 4 else None
                    yield 2.0
            for which in range(2):
                s, = use_pieces([2 + which])
                wv = wview(s, 8)
                prev = None
                for i in range(5):
                    if i < 4:
                        bk = next_bank()
                        for kc in range(8):
                            add("pe", (lambda e, i=i, kc=kc, bk=bk, wv=wv: e.matmul(
                                ps[:, bk, :], lhsT=actT[:, kc, i * 128:(i + 1) * 128], rhs=wv[:, kc, :],
                                start=(kc == 0), stop=(kc == 7))),
                                reads=[("wslot", s), ("actT", kc)], writes=[bank(bk)])
                    if prev is not None:
                        pi, pbk = prev
                        T = b * 4 + pi
                        if which == 0:
                            add("dve", (lambda e, T=T, pbk=pbk: e.tensor_copy(out=V[:, T, :], in_=ps[:, pbk, :])),
                                reads=[bank(pbk)], writes=[("V", T)])
                        else:
                            add("dve", (lambda e, T=T, pbk=pbk: e.tensor_copy(out=U[:, T % 5, :], in_=ps[:, pbk, :])),
                                reads=[bank(pbk)], writes=[("U", T % 5)])
                    prev = (i, bk) if i < 4 else None
                    yield 2.0
            for g in range(4):
                bk = next_bank()
                for i in range(4):
                    T = b * 4 + i
                    cur, prv = T % 5, (T - 1) % 5
                    cd = C_POOL + 3 * g + (2 if T == 0 else 0)
                    co = C_POOL + 3 * g + 1
                    add("pe", (lambda e, g=g, i=i, cur=cur, cd=cd, bk=bk: e.matmul(
                        ps[:, bk, i * 128:(i + 1) * 128], lhsT=U[:, cur, g * 128:(g + 1) * 128], rhs=cbf[:, cd, :],
                        start=True, stop=False)),
                        reads=[("U", cur), ("cbf",)], writes=[bank(bk)])
                    add("pe", (lambda e, g=g, i=i, prv=prv, co=co, bk=bk: e.matmul(
                        ps[:, bk, i * 128:(i + 1) * 128], lhsT=U[:, prv, g * 128:(g + 1) * 128], rhs=cbf[:, co, :],
                        start=False, stop=True)),
                        reads=[("U", prv), ("cbf",)], writes=[bank(bk)])
                add("dve", (lambda e, g=g, bk=bk: e.tensor_copy(out=dT[:, 0, :], in_=ps[:, bk, :])),
                    reads=[bank(bk)], writes=[("dT", 0)])
                bk2 = next_bank()
                add("pe", (lambda e, g=g, bk2=bk2: e.matmul(
                    ps[:, bk2, :], lhsT=wpool[:, g, :], rhs=dT[:, 0, :], start=True, stop=True)),
                    reads=[("dT", 0), ("wpool",)], writes=[bank(bk2)])
                add("dve", (lambda e, g=g, bk2=bk2: e.tensor_scalar(
                    out=mixP[:, par, g, :], in0=ps[:, bk2, :], scalar1=cols[:, 4 + g:5 + g], scalar2=None, op0=ALU.mult)),
                    reads=[bank(bk2), ("cols",)], writes=[("mixP", par, g)])
                yield 1.0

        def post(b):
            par = b % 2
            t0 = b * TB
            load_gb(1)
            s0, s1 = use_pieces([4, 5])
            wvs = [wview(s0, 8), wview(s1, 8)]
            for i in range(4):
                for half in range(2):
                    for kc in range(8):
                        lhs = mixA[:, kc, i * 128:(i + 1) * 128] if kc < 4 else mixP[:, par, kc - 4, i * 128:(i + 1) * 128]
                        rd = ("mixA", kc) if kc < 4 else ("mixP", par, kc - 4)
                        add("pe", (lambda e, lhs=lhs, half=half, kc=kc: e.matmul(
                            ps[:, 6 + half, :], lhsT=lhs, rhs=wvs[half][:, kc, :], start=(kc == 0), stop=(kc == 7))),
                            reads=[("wslot", (s0, s1)[half]), rd], writes=[bank(6 + half)])
                    yield 2.0
                add("dve", (lambda e, i=i: e.scalar_tensor_tensor(
                    out=xres[:, par, i, :].rearrange("p (a b) -> p a b", a=2),
                    in0=xres[:, par, i, :].rearrange("p (a b) -> p a b", a=2), scalar=ALPHA,
                    in1=ps[:, 6:8, :], op0=ALU.mult, op1=ALU.add)),
                    reads=[("xres", par, i), bank(6), bank(7)], writes=[("xres", par, i)])
                ln_stats(par, i)
                yield 2.5
            yield from ln_finish(par, (lambda i: dump(dbg_x1, par, i, t0, "dbg1")) if debug else None)
            relu_on_act[0] = (b <= 2) or (b == NBLK - 1)
            load_gb(2)
            for fh in range(2):
                prev = None
                for j in range(4):
                    s, = use_pieces([6 + fh * 4 + j])
                    wv = wview(s, 8)
                    for c in range(4):
                        fcl = 4 * j + c
                        bk = next_bank()
                        for kc in range(8):
                            add("pe", (lambda e, wv=wv, c=c, kc=kc, bk=bk: e.matmul(
                                ps[:, bk, :], lhsT=wv[:, kc, c * 128:(c + 1) * 128], rhs=actT[:, kc, :],
                                start=(kc == 0), stop=(kc == 7))),
                                reads=[("wslot", s), ("actT", kc)], writes=[bank(bk)])
                        if prev is not None:
                            relu2(*prev)
                        prev = (fcl, bk)
                        yield 2.0
                relu2(*prev)
                for oh in range(2):
                    for tp in range(2):
                        m0 = 14 + fh * 4 + oh * 2
                        sa, sb_ = use_pieces([m0, m0 + 1])
                        wq = [wview(sa, 8), wview(sb_, 8)]
                        for q in range(2):
                            for c in range(8):
                                fcl = q * 8 + c
                                for t in range(2):
                                    i = 2 * tp + t
                                    add("pe", (lambda e, wq=wq, q=q, c=c, fcl=fcl, i=i, t=t: e.matmul(
                                        ps[:, 6 + t, :], lhsT=hT[:, fcl, i * 128:(i + 1) * 128], rhs=wq[q][:, c, :],
                                        start=(fcl == 0), stop=(fcl == 15))),
                                        reads=[("wslot", (sa, sb_)[q]), HT(fcl)], writes=[bank(6 + t)])
                                if c % 4 == 3:
                                    yield 2.0
                        for t in range(2):
                            i = 2 * tp + t
                            xh = xres[:, par, i, oh * 512:(oh + 1) * 512]
                            if fh == 0:
                                add("dve", (lambda e, xh=xh, t=t: e.scalar_tensor_tensor(
                                    out=xh, in0=xh, scalar=ALPHA, in1=ps[:, 6 + t, :], op0=ALU.mult, op1=ALU.add)),
                                    reads=[("xres", par, i), bank(6 + t)], writes=[("xres", par, i)])
                            else:
                                add("dve", (lambda e, xh=xh, t=t: e.tensor_tensor(
                                    out=xh, in0=xh, in1=ps[:, 6 + t, :], op=ALU.add)),
                                    reads=[("xres", par, i), bank(6 + t)], writes=[("xres", par, i)])
                            if fh == 1 and oh == 1:
                                ln_stats(par, i)
                                yield 2.0
                            else:
                                yield 0.7
            yield from ln_finish(par, (lambda i: dump(dbg_x2, par, i, t0, "dbg2")) if debug else None)
            load_gb(3)
            for i in range(4):
                add("sp", (lambda e, i=i: e.dma_start(out=pbuf[:, i, :], in_=p_d[t0 + i * 128:t0 + (i + 1) * 128, :])),
                    writes=PBUF_R, dma_key=("pbuf", i))
            for c in range(2):
                bk = next_bank()
                for i in range(4):
                    add("pe", (lambda e, c=c, i=i, bk=bk: e.transpose(
                        out=ps[:, bk, i * 128:(i + 1) * 128], in_=pbuf[:, i, c * 128:(c + 1) * 128], identity=identf[:])),
                        reads=PBUF_R + [("identf",)], writes=[bank(bk)])
                add("dve", (lambda e, c=c, bk=bk: e.tensor_copy(out=pT[:, c, :], in_=ps[:, bk, :])),
                    reads=[bank(bk)], writes=PT_R)
            yield 1.0
            sg0, sg1, spl = use_pieces([22, 23, 24])
            wg = [wview(sg0, 8), wview(sg1, 8)]
            wpl = wsl[:, spl, 0:2048].rearrange("p (a b) -> p a b", a=2)
            egt = hflat[:, 3072:5120].bitcast(F32)
            egt3 = egt.rearrange("p (a b) -> p a b", a=2)
            EGT_R = EG_R + TMP_R
            for i in range(4):
                for oh in range(2):
                    for kc in range(8):
                        add("pe", (lambda e, i=i, oh=oh, kc=kc: e.matmul(
                            ps[:, 6 + oh, :], lhsT=actT[:, kc, i * 128:(i + 1) * 128], rhs=wg[oh][:, kc, :],
                            start=(kc == 0), stop=(kc == 7))),
                            reads=[("wslot", (sg0, sg1)[oh]), ("actT", kc)], writes=[bank(6 + oh)])
                    yield 2.0
                yield ("wait", 1)
                add("act", (lambda e: e.activation(out=egt3, in_=ps[:, 6:8, :], func=AF.Exp, scale=-1.0)),
                    reads=[bank(6), bank(7)], writes=EGT_R)
                add("act", (lambda e: e.activation(out=egt, in_=egt, func=AF.Ln, bias=1.0, scale=1.0)),
                    reads=EGT_R, writes=EGT_R)
                add("act", (lambda e: e.activation(out=egt, in_=egt, func=AF.Exp, scale=-1.0)),
                    reads=EGT_R, writes=EGT_R)
                yield 3.3
                yield ("wait", 1)
                for oh in range(2):
                    for c in range(2):
                        add("pe", (lambda e, i=i, oh=oh, c=c: e.matmul(
                            ps[:, 6 + oh, :], lhsT=pT[:, c, i * 128:(i + 1) * 128], rhs=wpl[:, c, oh * 512:(oh + 1) * 512],
                            start=(c == 0), stop=(c == 1))),
                            reads=[("wslot", spl)] + PT_R, writes=[bank(6 + oh)])
                add("dve", (lambda e: e.tensor_tensor(out=egt3, in0=ps[:, 6:8, :], in1=egt3, op=ALU.mult)),
                    reads=[bank(6), bank(7)] + EGT_R, writes=EGT_R)
                xt = xres[:, par, i, :]
                add("dve", (lambda e, xt=xt: e.scalar_tensor_tensor(
                    out=xt, in0=xt, scalar=ALPHA, in1=egt, op0=ALU.mult, op1=ALU.add)),
                    reads=[("xres", par, i)] + EGT_R, writes=[("xres", par, i)])
                ln_stats(par, i)
                yield 3.7
            def store(i):
                add("sp", (lambda e: e.dma_start(out=out_d[t0 + i * 128:t0 + (i + 1) * 128, :], in_=xres[:, par, i, :])),
                    reads=[("xres", par, i)], writes=[("outd", b, i)], dma_key=("xres_out", par, i))
            yield from ln_finish(par, store, do_T=False)

        relu_on_act = [False]

        def relu2(fcl, bk):
            rj = 0
            if relu_on_act[0]:
                add("act", (lambda e: e.activation(out=r32[:, rj, :], in_=ps[:, bk, :], func=AF.Relu)),
                    reads=[bank(bk)], writes=[("r32", rj)])
            else:
                add("dve", (lambda e: e.tensor_scalar(out=r32[:, rj, :], in0=ps[:, bk, :], scalar1=0.0, scalar2=None,
                                                      op0=ALU.max)),
                    reads=[bank(bk)], writes=[("r32", rj)])
            add("dve", (lambda e: e.tensor_tensor(out=hT[:, fcl, :], in0=r32[:, rj, :], in1=r32[:, rj, :], op=ALU.mult)),
                reads=[("r32", rj)], writes=[HT(fcl)])

        zc = [0]

        def attn(b):
            par = b % 2
            tiles = []
            for hp in range(4):
                nk = 4 * b + 4
                for j, kt in enumerate(range(nk - 1, -1, -1)):
                    diag = kt - 4 * b
                    tiles.append(dict(hp=hp, kt=kt, k=j, first=(j == 0), last=(j == nk - 1), diag=diag,
                                      c0=(diag * 128 if diag > 0 else 0), n=len(tiles)))
            N = len(tiles)

            def QK(t, bk0, first_use):
                hp, kt, c0 = t["hp"], t["kt"], t["c0"]
                for hh in range(2):
                    bk = bk0 + hh
                    add("pe", (lambda e, hp=hp, kt=kt, hh=hh, bk=bk, c0=c0: e.matmul(
                        ps[:, bk, c0:512], lhsT=KT[hh * 64:(hh + 1) * 64, hp, kt * 128:(kt + 1) * 128],
                        rhs=qT[hh * 64:(hh + 1) * 64, par, hp, c0:512], start=True, stop=False, skip_group_check=True)),
                        reads=[("KT", kt // 4, hp), ("qT", par, hp)], writes=[bank(bk)])
                if t["diag"] >= 0:
                    for hh in range(2):
                        bk = bk0 + hh
                        add("pe", (lambda e, bk=bk, c0=c0: e.matmul(
                            ps[:, bk, c0:c0 + 128], lhsT=cbf[:, C_IDENT, :], rhs=cbf[:, C_MASK, :],
                            start=False, stop=False, skip_group_check=True)),
                            reads=[("cbf",)], writes=[bank(bk)])

            def S1(t):
                t["zs"] = 0
                QK(t, 0, True)

            def S2a(t):
                c0 = t["c0"]
                add("act", (lambda e, c0=c0: e.activation(
                    out=e1[:, :, c0:512], in_=ps[:, 0:2, c0:512], func=AF.Exp)),
                    reads=[bank(0), bank(1)], writes=[("e",)])

            def S2b(t):
                n, c0 = t["n"], t["c0"]
                sj = n % 3
                if c0 > 0:
                    add("dve", (lambda e, sj=sj, c0=c0: e.memset(sp_r[:, sj, :, 0:c0], 0.0)), writes=[("sp", sj)])
                add("act", (lambda e, sj=sj, c0=c0: e.activation(
                    out=sp_r[:, sj, :, c0:512], in_=e1[:, :, c0:512], func=AF.Ln, bias=1.0, scale=1.0)),
                    reads=[("e",)], writes=[("sp", sj)])

            def r_src(t):
                k, n = t["k"], t["n"]
                if k == 0:
                    return None, None
                if k == 1:
                    return sp_r[:, (n - 1) % 3], ("sp", (n - 1) % 3)
                return R_r[:, k % 3], ("R", k % 3)

            def S3(t):
                k, n = t["k"], t["n"]
                if t["last"] or k == 0:
                    return
                src, sres = r_src(t)
                dj = (k + 1) % 3
                add("dve", (lambda e, src=src, n=n, dj=dj: e.tensor_tensor(
                    out=R_r[:, dj], in0=src, in1=sp_r[:, n % 3], op=ALU.add)),
                    reads=[sres, ("sp", n % 3)], writes=[("R", dj)])

            def S4(t):
                n, c0 = t["n"], t["c0"]
                src, sres = r_src(t)
                QK(t, 2, False)
                for hh in range(2):
                    bk = 2 + hh
                    add("pe", (lambda e, n=n, hh=hh, bk=bk, c0=c0: e.matmul(
                        ps[:, bk, c0:512], lhsT=cbf[:, C_TRI, :], rhs=sp_r[:, n % 3, hh, c0:512],
                        start=False, stop=(src is None), skip_group_check=True)),
                        reads=[("sp", n % 3), ("cbf",)], writes=[bank(bk)])
                    if src is not None:
                        add("pe", (lambda e, hh=hh, bk=bk, c0=c0: e.matmul(
                            ps[:, bk, c0:512], lhsT=cbf[:, C_ONES, :], rhs=src[:, hh, c0:512],
                            start=False, stop=True, skip_group_check=True)),
                            reads=[sres, ("cbf",)], writes=[bank(bk)])

            def S5(t):
                n, c0 = t["n"], t["c0"]
                wj = n % 2
                add("act", (lambda e, wj=wj, c0=c0: e.activation(
                    out=W_r[:, wj, :, c0:512], in_=ps[:, 2:4, c0:512], func=AF.Exp)),
                    reads=[bank(2), bank(3)], writes=[("W", wj)])

            def S6(t):
                n, hp, kt, c0 = t["n"], t["hp"], t["kt"], t["c0"]
                wj = n % 2
                ob = 4 + hp % 2
                for hh in range(2):
                    h = 2 * hp + hh
                    add("pe", (lambda e, kt=kt, h=h, hh=hh, wj=wj, ob=ob, c0=c0, t=t: e.matmul(
                        ps[hh * 64:(hh + 1) * 64, ob, c0:512], lhsT=V[:, kt, h * 64:(h + 1) * 64],
                        rhs=W_r[:, wj, hh, c0:512], start=t["first"], stop=t["last"], skip_group_check=True)),
                        reads=[("V", kt), ("W", wj)], writes=[bank(ob)])
                if t["last"]:
                    epilogue(hp, ob)

            def epilogue(hp, ob):
                mb = 4 + (hp + 1) % 2
                rj = 0
                add("act", (lambda e: e.activation(out=sqb, in_=ps[:, ob, :], func=AF.Square)),
                    reads=[bank(ob)], writes=[("dT", 0)])
                add("pe", (lambda e: e.matmul(ps[:, mb, :], lhsT=cbf[:, C_BLK, :], rhs=sqb, start=True, stop=True)),
                    reads=[("dT", 0), ("cbf",)], writes=[bank(mb)])
                add("act", (lambda e: e.activation(out=rst[:, rj, :], in_=ps[:, mb, :], func=AF.Ln,
                                                   bias=RMS_EPS, scale=1.0)),
                    reads=[bank(mb)], writes=[("r32", 0)])
                add("act", (lambda e: e.activation(out=rst[:, rj, :], in_=rst[:, rj, :], func=AF.Exp, scale=-0.5)),
                    reads=[("r32", 0)], writes=[("r32", 0)])
                add("dve", (lambda e: e.scalar_tensor_tensor(
                    out=mixA[:, hp, :], in0=ps[:, ob, :], scalar=cols[:, hp:hp + 1], in1=rst[:, rj, :],
                    op0=ALU.mult, op1=ALU.mult)),
                    reads=[bank(ob), ("r32", 0), ("cols",)], writes=[("mixA", hp)])

            S1(tiles[0])
            for n in range(N + 2):
                if 1 <= n <= N:
                    S4(tiles[n - 1])
                if n < N:
                    S2a(tiles[n])
                if n + 1 < N:
                    S1(tiles[n + 1])
                if n < N:
                    S2b(tiles[n])
                    S3(tiles[n])
                if 1 <= n <= N:
                    S5(tiles[n - 1])
                if 2 <= n <= N + 1:
                    S6(tiles[n - 2])
                yield 1.0

        PRE_W, POST_W = 62.0, 275.0

        def drive(main_gen, nsteps, side_gens, side_total):
            ready = [0] * len(side_gens)
            alive = [True] * len(side_gens)
            done_w = 0.0
            rr = 0
            for s_ in range(nsteps):
                next(main_gen)
                budget = max(1.5, (side_total - done_w) / max(1, nsteps - s_))
                spent = 0.0
                while spent < budget:
                    cand = [i for i in range(len(side_gens)) if alive[i] and ready[i] <= s_]
                    if not cand:
                        break
                    i = cand[rr % len(cand)]
                    rr += 1
                    try:
                        y = next(side_gens[i])
                    except StopIteration:
                        alive[i] = False
                        continue
                    if isinstance(y, tuple):
                        ready[i] = s_ + y[1]
                    else:
                        spent += y
                done_w += spent
            for _ in main_gen:
                pass
            for g_ in side_gens:
                for _ in g_:
                    pass

        for _ in pre(0):
            pass
        for b in range(NBLK):
            sides, tot = [], 0.0
            if b >= 1:
                sides.append(post(b - 1))
                tot += POST_W
            if b + 1 < NBLK:
                sides.append(pre(b + 1))
                tot += PRE_W
            if b <= 2:
                tot *= 0.5
            nsteps = 16 * (b + 1) + 2
            drive(attn(b), nsteps, [itertools.chain(*sides)] if sides else [], tot)
        for _ in post(NBLK - 1):
            pass

        sch.finalize()
        finals = [(ent[0], ent[1]) for key, ent in sch.dma_sems.items()
                  if isinstance(key, tuple) and key[0] in ("xres_out", "dbg0", "dbg1", "dbg2")]
        with nc.Block() as block:
            @block.tensor
            def _(eng):
                sch.emit_engine("pe", eng)

            @block.scalar
            def _(eng):
                sch.emit_engine("act", eng)

            @block.vector
            def _(eng):
                sch.emit_engine("dve", eng)

            @block.gpsimd
            def _(eng):
                sch.emit_engine("pool", eng)

            @block.sync
            def _(eng):
                sch.emit_engine("sp", eng, final_waits=finals)
    return nc


def _consts():
    bf = ml_dtypes.bfloat16
    cb = np.zeros((NCB, 128, 128), np.float32)
    j = np.arange(128)[:, None]
    s = np.arange(128)[None, :]
    cb[C_IDENT] = np.eye(128)
    cb[C_TRI] = -(j >= s).astype(np.float32)
    cb[C_ONES] = -1.0
    blk = np.zeros((128, 128), np.float32)
    blk[:64, :64] = 1.0 / 64
    blk[64:, 64:] = 1.0 / 64
    cb[C_BLK] = blk
    for g, w in enumerate(POOL_WINDOWS):
        t = s
        diag = ((j <= t) & (j > t - w)).astype(np.float32) / w - (j == t)
        off = ((j - 128) > (t - w)).astype(np.float32) / w
        cnt = np.minimum(t + 1, w).astype(np.float32)
        first = ((j <= t) & (j > t - w)).astype(np.float32) / cnt - (j == t)
        cb[C_POOL + 3 * g + 0] = diag
        cb[C_POOL + 3 * g + 1] = off
        cb[C_POOL + 3 * g + 2] = first
    cb[C_MASK] = np.where(j >= s, NEG, 0.0)
    cbf = np.ascontiguousarray(cb.transpose(1, 0, 2)).astype(bf)
    return cbf, np.eye(128, dtype=np.float32)


_NC_CACHE = {}


def _get_nc(S):
    if S not in _NC_CACHE:
        _NC_CACHE[S] = build_nc(S)
    return _NC_CACHE[S]


def make_in_maps(x, p, emb_ln_g, emb_ln_b, w_in, attn_out_g, w_pool, pool_scale, w_out,
                 ln1_g, ln1_b, w_up, w_down, ln2_g, ln2_b, w_ple, w_ple_gate, ln3_g, ln3_b):
    f = lambda a: np.ascontiguousarray(np.asarray(a, dtype=np.float32))
    B = x.shape[0]
    cbf, identf = _consts()
    lnp = np.stack([f(emb_ln_g), f(emb_ln_b), f(ln1_g)[0], f(ln1_b)[0], f(ln2_g)[0], f(ln2_b)[0],
                    f(ln3_g)[0], f(ln3_b)[0]], axis=0)
    cols = np.concatenate([f(attn_out_g)[0].reshape(4, 128).T, f(pool_scale)[0].reshape(4, 128).T], axis=1)
    shared = {
        "w_in": f(w_in)[0], "w_out": f(w_out)[0], "w_up": f(w_up)[0], "w_down": f(w_down)[0],
        "w_ple": f(w_ple)[0], "w_gate": f(w_ple_gate)[0], "w_pool": f(w_pool)[0],
        "lnp": np.ascontiguousarray(lnp), "cols": np.ascontiguousarray(cols),
        "cbf": cbf, "identf": identf,
    }
    xs = f(x)
    ps_ = f(p)[0]
    in_maps = []
    for c in range(B):
        m = dict(shared)
        m["x"] = xs[c]
        m["p"] = ps_[c]
        in_maps.append(m)
    return in_maps


def kernel(**inputs):
    x = inputs["x"]
    B, S, _ = x.shape
    nc = _get_nc(S)
    in_maps = make_in_maps(**inputs)
    res = run_bass_kernel_spmd(nc, in_maps, core_ids=list(range(B)))
    out = np.stack([np.asarray(r["out"], dtype=np.float32) for r in res.results], axis=0)
    return out
```

```python
import contextlib
import itertools
import numpy as np
import ml_dtypes
import concourse.bass as bass
import concourse.mybir as mybir
from concourse.bass_utils import run_bass_kernel_spmd

F32 = mybir.dt.float32
BF16 = mybir.dt.bfloat16
AF = mybir.ActivationFunctionType
ALU = mybir.AluOpType

D = 1024
DH = 64
DFF = 4096
PLE = 256
TB = 512
TT = 128
ALPHA = float(2.0 ** 0.25)
LN_EPS = 1e-5
RMS_EPS = 1e-6
NEG = -30000.0
POOL_WINDOWS = (2, 4, 8, 16)
NWSLOT = 3
NPIECE = 25

C_IDENT, C_TRI, C_ONES, C_BLK = 0, 1, 2, 3
C_POOL = 4
C_MASK = 16
NCB = 17


class Sched:
    def __init__(self, nc):
        self.nc = nc
        self.ops = []
        self.last_w = {}
        self.readers = {}
        self.eng_sem = {e: nc.alloc_semaphore("sem_" + e) for e in ("pe", "act", "dve", "pool")}
        self.dma_sems = {}

    def add(self, eng, fn, reads=(), writes=(), dma_key=None):
        oid = len(self.ops)
        deps = set()
        for r in reads:
            if r in self.last_w:
                deps.add(self.last_w[r])
        for w in writes:
            if w in self.last_w:
                deps.add(self.last_w[w])
            for rd in self.readers.get(w, ()):
                deps.add(rd)
        op = dict(id=oid, eng=eng, fn=fn, deps=deps, dma_key=dma_key, need_inc=False, cnt=0)
        if dma_key is not None:
            if dma_key not in self.dma_sems:
                self.dma_sems[dma_key] = [self.nc.alloc_semaphore("dsem%d" % len(self.dma_sems)), 0]
            ent = self.dma_sems[dma_key]
            ent[1] += 16
            op["dma_sem"] = ent[0]
            op["dma_val"] = ent[1]
        self.ops.append(op)
        for w in writes:
            self.last_w[w] = oid
            self.readers[w] = []
        for r in reads:
            self.readers.setdefault(r, []).append(oid)
        return oid

    def finalize(self):
        ops = self.ops
        for op in ops:
            for d in op["deps"]:
                B = ops[d]
                if B["dma_key"] is not None:
                    continue
                if B["eng"] == "pe" and op["eng"] == "pe" and op["dma_key"] is None:
                    continue
                B["need_inc"] = True
        cnt = {e: 0 for e in self.eng_sem}
        for op in ops:
            if op["dma_key"] is None and op["need_inc"]:
                cnt[op["eng"]] += 1
                op["cnt"] = cnt[op["eng"]]
        self.by_eng = {}
        for op in ops:
            self.by_eng.setdefault(op["eng"], []).append(op)

    def emit_engine(self, ename, eng, final_waits=()):
        ops = self.ops
        waited = {}
        for op in self.by_eng.get(ename, []):
            need = {}
            for d in op["deps"]:
                B = ops[d]
                if B["dma_key"] is not None:
                    sem, val = B["dma_sem"], B["dma_val"]
                else:
                    if B["eng"] == "pe" and ename == "pe" and op["dma_key"] is None:
                        continue
                    sem, val = self.eng_sem[B["eng"]], B["cnt"]
                k = sem.num
                if k not in need or need[k][1] < val:
                    need[k] = (sem, val)
            for k, (sem, val) in need.items():
                if waited.get(k, 0) >= val:
                    continue
                waited[k] = val
                eng.wait_ge(sem, val)
            ins = op["fn"](eng)
            if op["dma_key"] is not None:
                ins.then_inc(op["dma_sem"], 16)
            elif op["need_inc"]:
                ins.then_inc(self.eng_sem[ename], 1)
        for sem, val in final_waits:
            eng.wait_ge(sem, val)


def build_nc(S, debug=False):
    NBLK = S // TB
    NT = S // TT
    nc = bass.Bass("TRN2", target_bir_lowering=False)

    def din(name, shape, dt=F32):
        return nc.dram_tensor(name, shape, dt, kind="ExternalInput").ap()

    x_d = din("x", [S, D])
    p_d = din("p", [S, PLE])
    w_in_d = din("w_in", [D, 2048])
    w_out_d = din("w_out", [D, D])
    w_up_d = din("w_up", [D, DFF])
    w_down_d = din("w_down", [DFF, D])
    w_ple_d = din("w_ple", [PLE, D])
    w_gate_d = din("w_gate", [D, D])
    w_pool_d = din("w_pool", [4, 128, 128])
    lnp_d = din("lnp", [8, D])
    cols_d = din("cols", [128, 8])
    cbf_d = din("cbf", [128, NCB, 128], BF16)
    identf_d = din("identf", [128, 128])
    out_d = nc.dram_tensor("out", [S, D], F32, kind="ExternalOutput").ap()
    wsc_d = nc.dram_tensor("wsc", [NPIECE, 128, 4096], BF16, kind="Internal").ap()
    if debug:
        dbg_x0 = nc.dram_tensor("dbg_x0", [S, D], F32, kind="ExternalOutput").ap()
        dbg_x1 = nc.dram_tensor("dbg_x1", [S, D], F32, kind="ExternalOutput").ap()
        dbg_x2 = nc.dram_tensor("dbg_x2", [S, D], F32, kind="ExternalOutput").ap()

    es = contextlib.ExitStack()
    with es:
        def sb(name, shape, dt):
            return es.enter_context(nc.sbuf_tensor(name, shape, dt))

        KT = sb("KT", [128, 4, S], BF16)
        V = sb("V", [128, NT, 512], BF16)
        U = sb("U", [128, 5, 512], BF16)
        wsl = sb("wsl", [128, NWSLOT, 4096], BF16)
        gb = sb("gb", [128, 2, D], F32)
        xres = sb("xres", [128, 2, 4, D], F32)
        actT = sb("actT", [128, 8, 512], BF16)
        qT = sb("qT", [128, 2, 4, 512], BF16)
        mixA = sb("mixA", [128, 4, 512], BF16)
        mixP = sb("mixP", [128, 2, 4, 512], BF16)
        e1 = sb("e1", [128, 2, 512], F32)
        sp_r = sb("sp_r", [128, 3, 2, 512], BF16)
        R_r = sb("R_r", [128, 3, 2, 512], BF16)
        W_r = sb("W_r", [128, 2, 2, 512], BF16)
        hT = sb("hT", [128, 16, 512], BF16)
        r32 = sb("r32", [128, 1, 512], F32)
        rst = r32
        dT = sb("dT", [128, 1, 512], BF16)
        sqb = dT[:, 0, :]
        cbf = sb("cbf_s", [128, NCB, 128], BF16)
        identf = sb("identf_s", [128, 128], F32)
        wpool = sb("wpool_s", [128, 4, 128], BF16)
        cols = sb("cols_s", [128, 8], F32)
        st = sb("st", [128, 2, 4, 12], F32)
        mv = sb("mv", [128, 2, 4, 2], F32)
        sm = sb("sm", [128, 2, 4, 4], F32)
        ps = es.enter_context(nc.psum_tensor("ps", [128, 8, 512], F32))

        hflat = hT[:].rearrange("p a b -> p (a b)")
        pbuf = hflat[:, 0:2048].bitcast(F32).rearrange("p (a b) -> p a b", a=4)
        pT = hflat[:, 2048:3072].rearrange("p (a b) -> p a b", a=2)
        egb = hflat[:, 3072:4096].bitcast(F32)
        tmpb = hflat[:, 4096:5120].bitcast(F32)
        HT = lambda s_: ("hT", s_)
        PBUF_R = [HT(0), HT(1), HT(2), HT(3)]
        PT_R = [HT(4), HT(5)]
        EG_R = [HT(6), HT(7)]
        TMP_R = [HT(8), HT(9)]

        sch = Sched(nc)
        add = sch.add

        def bank(k):
            return ("bank", k)

        add("sp", lambda e: e.dma_start(out=cbf[:], in_=cbf_d), writes=[("cbf",)], dma_key="cbf")
        add("sp", lambda e: e.dma_start(out=identf[:], in_=identf_d), writes=[("identf",)], dma_key="identf")
        add("sp", lambda e: e.dma_start(out=cols[:], in_=cols_d), writes=[("cols",)], dma_key="cols")
        add("pool", lambda e: e.dma_start(out=wpool[:], in_=w_pool_d.rearrange("g c d -> c g d")),
            writes=[("wpool",)], dma_key="wpool")
        add("pool", lambda e: e.memset(U[:, 4, :], 0.0), writes=[("U", 4)])

        def piece_src(n):
            if n < 4:
                return w_in_d.rearrange("(kc p) f -> p kc f", p=128)[:, :, n * 512:(n + 1) * 512], 8
            if n < 6:
                h = n - 4
                return w_out_d.rearrange("(kc p) f -> p kc f", p=128)[:, :, h * 512:(h + 1) * 512], 8
            if n < 14:
                j = n - 6
                return w_up_d.rearrange("(kc p) f -> p kc f", p=128)[:, :, j * 512:(j + 1) * 512], 8
            if n < 22:
                m = n - 14
                fh, oh, q = m // 4, (m // 2) % 2, m % 2
                c0 = fh * 16 + q * 8
                return w_down_d.rearrange("(c p) o -> p c o", p=128)[:, c0:c0 + 8, oh * 512:(oh + 1) * 512], 8
            if n < 24:
                h = n - 22
                return w_gate_d.rearrange("(kc p) f -> p kc f", p=128)[:, :, h * 512:(h + 1) * 512], 8
            return w_ple_d.rearrange("(c p) o -> p c o", p=128), 2

        for n in range(NPIECE):
            src, a = piece_src(n)
            if n < 24:
                dst = wsc_d[n].rearrange("p (a b) -> p a b", a=a)
            else:
                dst = wsc_d[n][:, 0:2048].rearrange("p (a b) -> p a b", a=2)
            add("pool", (lambda e, dst=dst, src=src: e.dma_start(out=dst, in_=src)),
                writes=[("wsc", n)], dma_key=("wsc", n))

        PRE_SEQ = [0, 1, 2, 3]
        POST_SEQ = [4, 5]
        for fh in range(2):
            POST_SEQ += [6 + fh * 4 + j for j in range(4)]
            for oh in range(2):
                for tp in range(2):
                    POST_SEQ += [14 + fh * 4 + oh * 2, 14 + fh * 4 + oh * 2 + 1]
        POST_SEQ += [22, 23, 24]
        full_seq = list(PRE_SEQ)
        for b in range(NBLK):
            if b >= 1:
                full_seq += POST_SEQ
            if b + 1 < NBLK:
                full_seq += PRE_SEQ
        POST_SEQ_TAIL = [4, 5]
        for fh in range(2):
            POST_SEQ_TAIL += [6 + fh * 4 + j for j in range(4)]
            for oh in range(2):
                POST_SEQ_TAIL += [14 + fh * 4 + oh * 2, 14 + fh * 4 + oh * 2 + 1]
        POST_SEQ_TAIL += [22, 23, 24]
        full_seq += POST_SEQ_TAIL
        wstate = {"issued": 0, "ptr": 0}

        def issue_upto(g):
            while wstate["issued"] <= min(g, len(full_seq) - 1):
                gi = wstate["issued"]
                n = full_seq[gi]
                s = gi % NWSLOT
                if n < 24:
                    o_ap, i_ap = wsl[:, s, :], wsc_d[n]
                else:
                    o_ap, i_ap = wsl[:, s, 0:2048], wsc_d[n][:, 0:2048]
                add("sp", (lambda e, o_ap=o_ap, i_ap=i_ap: e.dma_start(out=o_ap, in_=i_ap)),
                    reads=[("wsc", n)], writes=[("wslot", s)], dma_key=("wslot", s))
                wstate["issued"] += 1

        def use_pieces(expect):
            k = len(expect)
            assert k <= NWSLOT
            g = wstate["ptr"]
            assert full_seq[g:g + k] == list(expect), (full_seq[g:g + k], expect)
            wstate["ptr"] += k
            issue_upto(g + NWSLOT - 1)
            return [(g + j) % NWSLOT for j in range(k)]

        def wview(s, a):
            return wsl[:, s, :].rearrange("p (a b) -> p a b", a=a)

        sbc = [0]

        def next_bank():
            sbc[0] += 1
            return 6 + sbc[0] % 2

        def load_gb(k):
            for j in range(2):
                add("sp", (lambda e, j=j, k=k: e.dma_start(
                    out=gb[:, j, :], in_=lnp_d[2 * k + j:2 * k + j + 1, :].partition_broadcast(128))),
                    writes=[("gb", j)], dma_key=("gb", j))

        def ln_stats(par, i):
            xr = ("xres", par, i)
            add("dve", lambda e: e.bn_stats(out=st[:, par, i, 0:6], in_=xres[:, par, i, 0:512]),
                reads=[xr], writes=[("st", par, i, 0)])
            add("dve", lambda e: e.bn_stats(out=st[:, par, i, 6:12], in_=xres[:, par, i, 512:1024]),
                reads=[xr], writes=[("st", par, i, 1)])
            add("dve", lambda e: e.bn_aggr(out=mv[:, par, i, :], in_=st[:, par, i, :]),
                reads=[("st", par, i, 0), ("st", par, i, 1)], writes=[("mv", par, i)])

        def transpose_tile(par, i):
            for c in range(8):
                add("pe", (lambda e, c=c: e.transpose(
                    out=ps[:, 6 + c // 4, (c % 4) * 128:(c % 4 + 1) * 128], in_=xres[:, par, i, c * 128:(c + 1) * 128],
                    identity=identf[:])),
                    reads=[("xres", par, i), ("identf",)], writes=[bank(6 + c // 4)])
            for h in range(2):
                add("dve", (lambda e, h=h: e.tensor_copy(
                    out=actT[:, 4 * h:4 * h + 4, i * 128:(i + 1) * 128],
                    in_=ps[:, 6 + h, :].rearrange("p (a b) -> p a b", a=4))),
                    reads=[bank(6 + h)], writes=[("actT", c) for c in range(4 * h, 4 * h + 4)])

        def ln_finish(par, after=None, do_T=True):
            yield ("wait", 1)
            for i in range(4):
                add("act", lambda e, i=i: e.activation(out=sm[:, par, i, 0:1], in_=mv[:, par, i, 1:2], func=AF.Ln,
                                                       bias=LN_EPS, scale=1.0),
                    reads=[("mv", par, i)], writes=[("sm", par, i, 0)])
                add("act", lambda e, i=i: e.activation(out=sm[:, par, i, 1:2], in_=sm[:, par, i, 0:1], func=AF.Exp,
                                                       scale=-0.5),
                    reads=[("sm", par, i, 0)], writes=[("sm", par, i, 1)])
            yield 0.6
            yield ("wait", 1)
            for i in range(4):
                xr = ("xres", par, i)
                xt = xres[:, par, i, :]
                add("dve", lambda e, i=i: e.tensor_scalar(out=sm[:, par, i, 2:3], in0=mv[:, par, i, 0:1],
                                                          scalar1=sm[:, par, i, 1:2], scalar2=-1.0,
                                                          op0=ALU.mult, op1=ALU.mult),
                    reads=[("mv", par, i), ("sm", par, i, 1)], writes=[("sm", par, i, 2)])
                add("dve", lambda e, i=i, xt=xt: e.tensor_scalar(out=xt, in0=xt, scalar1=sm[:, par, i, 1:2],
                                                                 scalar2=sm[:, par, i, 2:3], op0=ALU.mult, op1=ALU.add),
                    reads=[xr, ("sm", par, i, 1), ("sm", par, i, 2)], writes=[xr])
                yield 0.8
                add("dve", lambda e, xt=xt: e.tensor_tensor(out=xt, in0=xt, in1=gb[:, 0, :], op=ALU.mult),
                    reads=[xr, ("gb", 0)], writes=[xr])
                yield 1.2
                add("dve", lambda e, xt=xt: e.tensor_tensor(out=xt, in0=xt, in1=gb[:, 1, :], op=ALU.add),
                    reads=[xr, ("gb", 1)], writes=[xr])
                if after is not None:
                    after(i)
                yield 1.2
                if do_T and i >= 1:
                    transpose_tile(par, i - 1)
                    yield 2.5
            if do_T:
                yield ("wait", 1)
                transpose_tile(par, 3)
                yield 2.5

        def dump(dst, par, i, t0, key):
            add("sp", (lambda e: e.dma_start(out=dst[t0 + i * 128:t0 + (i + 1) * 128, :], in_=xres[:, par, i, :])),
                reads=[("xres", par, i)], dma_key=(key, par, i))

        def to_actT(par):
            prev = None
            for c in range(8):
                bk = next_bank()
                for i in range(4):
                    add("pe", (lambda e, c=c, i=i, bk=bk: e.transpose(
                        out=ps[:, bk, i * 128:(i + 1) * 128], in_=xres[:, par, i, c * 128:(c + 1) * 128],
                        identity=identf[:])),
                        reads=[("xres", par, i), ("identf",)], writes=[bank(bk)])
                if prev is not None:
                    pc, pbk = prev
                    add("dve", (lambda e, pc=pc, pbk=pbk: e.tensor_copy(out=actT[:, pc, :], in_=ps[:, pbk, :])),
                        reads=[bank(pbk)], writes=[("actT", pc)])
                prev = (c, bk)
                yield 1.8
            pc, pbk = prev
            add("dve", (lambda e: e.tensor_copy(out=actT[:, pc, :], in_=ps[:, pbk, :])),
                reads=[bank(pbk)], writes=[("actT", pc)])

        def pre(b):
            par = b % 2
            t0 = b * TB
            load_gb(0)
            for i in range(4):
                add("sp", (lambda e, i=i: e.dma_start(out=xres[:, par, i, :], in_=x_d[t0 + i * 128:t0 + (i + 1) * 128, :])),
                    writes=[("xres", par, i)], dma_key=("xres", par, i))
            for i in range(4):
                ln_stats(par, i)
                yield 1.3
            yield from ln_finish(par, (lambda i: dump(dbg_x0, par, i, t0, "dbg0")) if debug else None)
            for which in range(2):
                s, = use_pieces([which])
                wv = wview(s, 8)
                prev = None
                for hp in range(5):
                    if hp < 4:
                        bk = next_bank()
                        for kc in range(8):
                            add("pe", (lambda e, hp=hp, kc=kc, bk=bk, wv=wv: e.matmul(
                                ps[:, bk, :], lhsT=wv[:, kc, hp * 128:(hp + 1) * 128], rhs=actT[:, kc, :],
                                start=(kc == 0), stop=(kc == 7))),
                                reads=[("wslot", s), ("actT", kc)], writes=[bank(bk)])
                    if prev is not None:
                        php, pbk = prev
                        if which == 0:
                            add("dve", (lambda e, php=php, pbk=pbk: e.tensor_scalar(
                                out=qT[:, par, php, :], in0=ps[:, pbk, :], scalar1=0.125, scalar2=None, op0=ALU.mult)),
                                reads=[bank(pbk)], writes=[("qT", par, php)])
                        else:
                            add("dve", (lambda e, php=php, pbk=pbk: e.tensor_copy(
                                out=KT[:, php, t0:t0 + TB], in_=ps[:, pbk, :])),
                                reads=[bank(pbk)], writes=[("KT", b, php)])
                    prev = (hp, bk) if hp < 4 else None
                    yield 2.0
            for which in range(2):
                s, = use_pieces([2 + which])
                wv = wview(s, 8)
                prev = None
                for i in range(5):
                    if i < 4:
                        bk = next_bank()
                        for kc in range(8):
                            add("pe", (lambda e, i=i, kc=kc, bk=bk, wv=wv: e.matmul(
                                ps[:, bk, :], lhsT=actT[:, kc, i * 128:(i + 1) * 128], rhs=wv[:, kc, :],
                                start=(kc == 0), stop=(kc == 7))),
                                reads=[("wslot", s), ("actT", kc)], writes=[bank(bk)])
                    if prev is not None:
                        pi, pbk = prev
                        T = b * 4 + pi
                        if which == 0:
                            add("dve", (lambda e, T=T, pbk=pbk: e.tensor_copy(out=V[:, T, :], in_=ps[:, pbk, :])),
                                reads=[bank(pbk)], writes=[("V", T)])
                        else:
                            add("dve", (lambda e, T=T, pbk=pbk: e.tensor_copy(out=U[:, T % 5, :], in_=ps[:, pbk, :])),
                                reads=[bank(pbk)], writes=[("U", T % 5)])
                    prev = (i, bk) if i < 4 else None
                    yield 2.0
            for g in range(4):
                bk = next_bank()
                for i in range(4):
                    T = b * 4 + i
                    cur, prv = T % 5, (T - 1) % 5
                    cd = C_POOL + 3 * g + (2 if T == 0 else 0)
                    co = C_POOL + 3 * g + 1
                    add("pe", (lambda e, g=g, i=i, cur=cur, cd=cd, bk=bk: e.matmul(
                        ps[:, bk, i * 128:(i + 1) * 128], lhsT=U[:, cur, g * 128:(g + 1) * 128], rhs=cbf[:, cd, :],
                        start=True, stop=False)),
                        reads=[("U", cur), ("cbf",)], writes=[bank(bk)])
                    add("pe", (lambda e, g=g, i=i, prv=prv, co=co, bk=bk: e.matmul(
                        ps[:, bk, i * 128:(i + 1) * 128], lhsT=U[:, prv, g * 128:(g + 1) * 128], rhs=cbf[:, co, :],
                        start=False, stop=True)),
                        reads=[("U", prv), ("cbf",)], writes=[bank(bk)])
                add("dve", (lambda e, g=g, bk=bk: e.tensor_copy(out=dT[:, 0, :], in_=ps[:, bk, :])),
                    reads=[bank(bk)], writes=[("dT", 0)])
                bk2 = next_bank()
                add("pe", (lambda e, g=g, bk2=bk2: e.matmul(
                    ps[:, bk2, :], lhsT=wpool[:, g, :], rhs=dT[:, 0, :], start=True, stop=True)),
                    reads=[("dT", 0), ("wpool",)], writes=[bank(bk2)])
                add("dve", (lambda e, g=g, bk2=bk2: e.tensor_scalar(
                    out=mixP[:, par, g, :], in0=ps[:, bk2, :], scalar1=cols[:, 4 + g:5 + g], scalar2=None, op0=ALU.mult)),
                    reads=[bank(bk2), ("cols",)], writes=[("mixP", par, g)])
                yield 1.0

        def post(b):
            par = b % 2
            t0 = b * TB
            load_gb(1)
            s0, s1 = use_pieces([4, 5])
            wvs = [wview(s0, 8), wview(s1, 8)]
            for i in range(4):
                for half in range(2):
                    for kc in range(8):
                        lhs = mixA[:, kc, i * 128:(i + 1) * 128] if kc < 4 else mixP[:, par, kc - 4, i * 128:(i + 1) * 128]
                        rd = ("mixA", kc) if kc < 4 else ("mixP", par, kc - 4)
                        add("pe", (lambda e, lhs=lhs, half=half, kc=kc: e.matmul(
                            ps[:, 6 + half, :], lhsT=lhs, rhs=wvs[half][:, kc, :], start=(kc == 0), stop=(kc == 7))),
                            reads=[("wslot", (s0, s1)[half]), rd], writes=[bank(6 + half)])
                    yield 2.0
                add("dve", (lambda e, i=i: e.scalar_tensor_tensor(
                    out=xres[:, par, i, :].rearrange("p (a b) -> p a b", a=2),
                    in0=xres[:, par, i, :].rearrange("p (a b) -> p a b", a=2), scalar=ALPHA,
                    in1=ps[:, 6:8, :], op0=ALU.mult, op1=ALU.add)),
                    reads=[("xres", par, i), bank(6), bank(7)], writes=[("xres", par, i)])
                ln_stats(par, i)
                yield 2.5
            yield from ln_finish(par, (lambda i: dump(dbg_x1, par, i, t0, "dbg1")) if debug else None)
            relu_on_act[0] = (b <= 2) or (b == NBLK - 1)
            load_gb(2)
            for fh in range(2):
                prev = None
                for j in range(4):
                    s, = use_pieces([6 + fh * 4 + j])
                    wv = wview(s, 8)
                    for c in range(4):
                        fcl = 4 * j + c
                        bk = next_bank()
                        for kc in range(8):
                            add("pe", (lambda e, wv=wv, c=c, kc=kc, bk=bk: e.matmul(
                                ps[:, bk, :], lhsT=wv[:, kc, c * 128:(c + 1) * 128], rhs=actT[:, kc, :],
                                start=(kc == 0), stop=(kc == 7))),
                                reads=[("wslot", s), ("actT", kc)], writes=[bank(bk)])
                        if prev is not None:
                            relu2(*prev)
                        prev = (fcl, bk)
                        yield 2.0
                relu2(*prev)
                for oh in (range(2) if b == NBLK - 1 else ()):
                    m0 = 14 + fh * 4 + oh * 2
                    sa, sb_ = use_pieces([m0, m0 + 1])
                    wq = [wview(sa, 8), wview(sb_, 8)]
                    base = 4 * ((fh * 2 + oh) % 2)
                    for q in range(2):
                        for c in range(8):
                            fcl = q * 8 + c
                            for i in range(4):
                                add("pe", (lambda e, wq=wq, q=q, c=c, fcl=fcl, i=i, base=base: e.matmul(
                                    ps[:, base + i, :], lhsT=hT[:, fcl, i * 128:(i + 1) * 128], rhs=wq[q][:, c, :],
                                    start=(fcl == 0), stop=(fcl == 15))),
                                    reads=[("wslot", (sa, sb_)[q]), HT(fcl)], writes=[bank(base + i)])
                            if c % 2 == 1:
                                yield 2.0
                    for i in range(4):
                        xh = xres[:, par, i, oh * 512:(oh + 1) * 512]
                        if fh == 0:
                            add("dve", (lambda e, xh=xh, i=i, base=base: e.scalar_tensor_tensor(
                                out=xh, in0=xh, scalar=ALPHA, in1=ps[:, base + i, :], op0=ALU.mult, op1=ALU.add)),
                                reads=[("xres", par, i), bank(base + i)], writes=[("xres", par, i)])
                        else:
                            add("dve", (lambda e, xh=xh, i=i, base=base: e.tensor_tensor(
                                out=xh, in0=xh, in1=ps[:, base + i, :], op=ALU.add)),
                                reads=[("xres", par, i), bank(base + i)], writes=[("xres", par, i)])
                        if fh == 1 and oh == 1:
                            ln_stats(par, i)
                        yield 0.7
                for oh in (range(2) if b != NBLK - 1 else ()):
                    for tp in range(2):
                        m0 = 14 + fh * 4 + oh * 2
                        sa, sb_ = use_pieces([m0, m0 + 1])
                        wq = [wview(sa, 8), wview(sb_, 8)]
                        for q in range(2):
                            for c in range(8):
                                fcl = q * 8 + c
                                for t in range(2):
                                    i = 2 * tp + t
                                    add("pe", (lambda e, wq=wq, q=q, c=c, fcl=fcl, i=i, t=t: e.matmul(
                                        ps[:, 6 + t, :], lhsT=hT[:, fcl, i * 128:(i + 1) * 128], rhs=wq[q][:, c, :],
                                        start=(fcl == 0), stop=(fcl == 15))),
                                        reads=[("wslot", (sa, sb_)[q]), HT(fcl)], writes=[bank(6 + t)])
                                if c % 4 == 3:
                                    yield 2.0
                        for t in range(2):
                            i = 2 * tp + t
                            xh = xres[:, par, i, oh * 512:(oh + 1) * 512]
                            if fh == 0:
                                add("dve", (lambda e, xh=xh, t=t: e.scalar_tensor_tensor(
                                    out=xh, in0=xh, scalar=ALPHA, in1=ps[:, 6 + t, :], op0=ALU.mult, op1=ALU.add)),
                                    reads=[("xres", par, i), bank(6 + t)], writes=[("xres", par, i)])
                            else:
                                add("dve", (lambda e, xh=xh, t=t: e.tensor_tensor(
                                    out=xh, in0=xh, in1=ps[:, 6 + t, :], op=ALU.add)),
                                    reads=[("xres", par, i), bank(6 + t)], writes=[("xres", par, i)])
                            if fh == 1 and oh == 1:
                                ln_stats(par, i)
                                yield 2.0
                            else:
                                yield 0.7
            yield from ln_finish(par, (lambda i: dump(dbg_x2, par, i, t0, "dbg2")) if debug else None)
            load_gb(3)
            for i in range(4):
                add("sp", (lambda e, i=i: e.dma_start(out=pbuf[:, i, :], in_=p_d[t0 + i * 128:t0 + (i + 1) * 128, :])),
                    writes=PBUF_R, dma_key=("pbuf", i))
            for c in range(2):
                bk = next_bank()
                for i in range(4):
                    add("pe", (lambda e, c=c, i=i, bk=bk: e.transpose(
                        out=ps[:, bk, i * 128:(i + 1) * 128], in_=pbuf[:, i, c * 128:(c + 1) * 128], identity=identf[:])),
                        reads=PBUF_R + [("identf",)], writes=[bank(bk)])
                add("dve", (lambda e, c=c, bk=bk: e.tensor_copy(out=pT[:, c, :], in_=ps[:, bk, :])),
                    reads=[bank(bk)], writes=PT_R)
            yield 1.0
            sg0, sg1, spl = use_pieces([22, 23, 24])
            wg = [wview(sg0, 8), wview(sg1, 8)]
            wpl = wsl[:, spl, 0:2048].rearrange("p (a b) -> p a b", a=2)
            egt = hflat[:, 3072:5120].bitcast(F32)
            egt3 = egt.rearrange("p (a b) -> p a b", a=2)
            EGT_R = EG_R + TMP_R
            for i in range(4):
                for oh in range(2):
                    for kc in range(8):
                        add("pe", (lambda e, i=i, oh=oh, kc=kc: e.matmul(
                            ps[:, 6 + oh, :], lhsT=actT[:, kc, i * 128:(i + 1) * 128], rhs=wg[oh][:, kc, :],
                            start=(kc == 0), stop=(kc == 7))),
                            reads=[("wslot", (sg0, sg1)[oh]), ("actT", kc)], writes=[bank(6 + oh)])
                    yield 2.0
                yield ("wait", 1)
                add("act", (lambda e: e.activation(out=egt3, in_=ps[:, 6:8, :], func=AF.Exp, scale=-1.0)),
                    reads=[bank(6), bank(7)], writes=EGT_R)
                add("act", (lambda e: e.activation(out=egt, in_=egt, func=AF.Ln, bias=1.0, scale=1.0)),
                    reads=EGT_R, writes=EGT_R)
                add("act", (lambda e: e.activation(out=egt, in_=egt, func=AF.Exp, scale=-1.0)),
                    reads=EGT_R, writes=EGT_R)
                yield 3.3
                yield ("wait", 1)
                for oh in range(2):
                    for c in range(2):
                        add("pe", (lambda e, i=i, oh=oh, c=c: e.matmul(
                            ps[:, 6 + oh, :], lhsT=pT[:, c, i * 128:(i + 1) * 128], rhs=wpl[:, c, oh * 512:(oh + 1) * 512],
                            start=(c == 0), stop=(c == 1))),
                            reads=[("wslot", spl)] + PT_R, writes=[bank(6 + oh)])
                add("dve", (lambda e: e.tensor_tensor(out=egt3, in0=ps[:, 6:8, :], in1=egt3, op=ALU.mult)),
                    reads=[bank(6), bank(7)] + EGT_R, writes=EGT_R)
                xt = xres[:, par, i, :]
                add("dve", (lambda e, xt=xt: e.scalar_tensor_tensor(
                    out=xt, in0=xt, scalar=ALPHA, in1=egt, op0=ALU.mult, op1=ALU.add)),
                    reads=[("xres", par, i)] + EGT_R, writes=[("xres", par, i)])
                ln_stats(par, i)
                yield 3.7
            def store(i):
                add("sp", (lambda e: e.dma_start(out=out_d[t0 + i * 128:t0 + (i + 1) * 128, :], in_=xres[:, par, i, :])),
                    reads=[("xres", par, i)], writes=[("outd", b, i)], dma_key=("xres_out", par, i))
            yield from ln_finish(par, store, do_T=False)

        relu_on_act = [False]

        def relu2(fcl, bk):
            rj = 0
            if relu_on_act[0]:
                add("act", (lambda e: e.activation(out=r32[:, rj, :], in_=ps[:, bk, :], func=AF.Relu)),
                    reads=[bank(bk)], writes=[("r32", rj)])
            else:
                add("dve", (lambda e: e.tensor_scalar(out=r32[:, rj, :], in0=ps[:, bk, :], scalar1=0.0, scalar2=None,
                                                      op0=ALU.max)),
                    reads=[bank(bk)], writes=[("r32", rj)])
            add("dve", (lambda e: e.tensor_tensor(out=hT[:, fcl, :], in0=r32[:, rj, :], in1=r32[:, rj, :], op=ALU.mult)),
                reads=[("r32", rj)], writes=[HT(fcl)])

        zc = [0]

        def attn(b):
            par = b % 2
            tiles = []
            for hp in range(4):
                nk = 4 * b + 4
                for j, kt in enumerate(range(nk - 1, -1, -1)):
                    diag = kt - 4 * b
                    tiles.append(dict(hp=hp, kt=kt, k=j, first=(j == 0), last=(j == nk - 1), diag=diag,
                                      c0=(diag * 128 if diag > 0 else 0), n=len(tiles)))
            N = len(tiles)

            def QK(t, bk0, first_use):
                hp, kt, c0 = t["hp"], t["kt"], t["c0"]
                for hh in range(2):
                    bk = bk0 + hh
                    add("pe", (lambda e, hp=hp, kt=kt, hh=hh, bk=bk, c0=c0: e.matmul(
                        ps[:, bk, c0:512], lhsT=KT[hh * 64:(hh + 1) * 64, hp, kt * 128:(kt + 1) * 128],
                        rhs=qT[hh * 64:(hh + 1) * 64, par, hp, c0:512], start=True, stop=False, skip_group_check=True)),
                        reads=[("KT", kt // 4, hp), ("qT", par, hp)], writes=[bank(bk)])
                if t["diag"] >= 0:
                    for hh in range(2):
                        bk = bk0 + hh
                        add("pe", (lambda e, bk=bk, c0=c0: e.matmul(
                            ps[:, bk, c0:c0 + 128], lhsT=cbf[:, C_IDENT, :], rhs=cbf[:, C_MASK, :],
                            start=False, stop=False, skip_group_check=True)),
                            reads=[("cbf",)], writes=[bank(bk)])

            def S1(t):
                t["zs"] = 0
                QK(t, 0, True)

            def S2a(t):
                c0 = t["c0"]
                add("act", (lambda e, c0=c0: e.activation(
                    out=e1[:, :, c0:512], in_=ps[:, 0:2, c0:512], func=AF.Exp)),
                    reads=[bank(0), bank(1)], writes=[("e",)])

            def S2b(t):
                n, c0 = t["n"], t["c0"]
                sj = n % 3
                if c0 > 0:
                    add("dve", (lambda e, sj=sj, c0=c0: e.memset(sp_r[:, sj, :, 0:c0], 0.0)), writes=[("sp", sj)])
                add("act", (lambda e, sj=sj, c0=c0: e.activation(
                    out=sp_r[:, sj, :, c0:512], in_=e1[:, :, c0:512], func=AF.Ln, bias=1.0, scale=1.0)),
                    reads=[("e",)], writes=[("sp", sj)])

            def r_src(t):
                k, n = t["k"], t["n"]
                if k == 0:
                    return None, None
                if k == 1:
                    return sp_r[:, (n - 1) % 3], ("sp", (n - 1) % 3)
                return R_r[:, k % 3], ("R", k % 3)

            def S3(t):
                k, n = t["k"], t["n"]
                if t["last"] or k == 0:
                    return
                src, sres = r_src(t)
                dj = (k + 1) % 3
                add("dve", (lambda e, src=src, n=n, dj=dj: e.tensor_tensor(
                    out=R_r[:, dj], in0=src, in1=sp_r[:, n % 3], op=ALU.add)),
                    reads=[sres, ("sp", n % 3)], writes=[("R", dj)])

            def S4(t):
                n, c0 = t["n"], t["c0"]
                src, sres = r_src(t)
                QK(t, 2, False)
                for hh in range(2):
                    bk = 2 + hh
                    add("pe", (lambda e, n=n, hh=hh, bk=bk, c0=c0: e.matmul(
                        ps[:, bk, c0:512], lhsT=cbf[:, C_TRI, :], rhs=sp_r[:, n % 3, hh, c0:512],
                        start=False, stop=(src is None), skip_group_check=True)),
                        reads=[("sp", n % 3), ("cbf",)], writes=[bank(bk)])
                    if src is not None:
                        add("pe", (lambda e, hh=hh, bk=bk, c0=c0: e.matmul(
                            ps[:, bk, c0:512], lhsT=cbf[:, C_ONES, :], rhs=src[:, hh, c0:512],
                            start=False, stop=True, skip_group_check=True)),
                            reads=[sres, ("cbf",)], writes=[bank(bk)])

            def S5(t):
                n, c0 = t["n"], t["c0"]
                wj = n % 2
                add("act", (lambda e, wj=wj, c0=c0: e.activation(
                    out=W_r[:, wj, :, c0:512], in_=ps[:, 2:4, c0:512], func=AF.Exp)),
                    reads=[bank(2), bank(3)], writes=[("W", wj)])

            def S6(t):
                n, hp, kt, c0 = t["n"], t["hp"], t["kt"], t["c0"]
                wj = n % 2
                ob = 4 + hp % 2
                for hh in range(2):
                    h = 2 * hp + hh
                    add("pe", (lambda e, kt=kt, h=h, hh=hh, wj=wj, ob=ob, c0=c0, t=t: e.matmul(
                        ps[hh * 64:(hh + 1) * 64, ob, c0:512], lhsT=V[:, kt, h * 64:(h + 1) * 64],
                        rhs=W_r[:, wj, hh, c0:512], start=t["first"], stop=t["last"], skip_group_check=True)),
                        reads=[("V", kt), ("W", wj)], writes=[bank(ob)])
                if t["last"]:
                    epilogue(hp, ob)

            def epilogue(hp, ob):
                mb = 4 + (hp + 1) % 2
                rj = 0
                add("act", (lambda e: e.activation(out=sqb, in_=ps[:, ob, :], func=AF.Square)),
                    reads=[bank(ob)], writes=[("dT", 0)])
                add("pe", (lambda e: e.matmul(ps[:, mb, :], lhsT=cbf[:, C_BLK, :], rhs=sqb, start=True, stop=True)),
                    reads=[("dT", 0), ("cbf",)], writes=[bank(mb)])
                add("act", (lambda e: e.activation(out=rst[:, rj, :], in_=ps[:, mb, :], func=AF.Ln,
                                                   bias=RMS_EPS, scale=1.0)),
                    reads=[bank(mb)], writes=[("r32", 0)])
                add("act", (lambda e: e.activation(out=rst[:, rj, :], in_=rst[:, rj, :], func=AF.Exp, scale=-0.5)),
                    reads=[("r32", 0)], writes=[("r32", 0)])
                add("dve", (lambda e: e.scalar_tensor_tensor(
                    out=mixA[:, hp, :], in0=ps[:, ob, :], scalar=cols[:, hp:hp + 1], in1=rst[:, rj, :],
                    op0=ALU.mult, op1=ALU.mult)),
                    reads=[bank(ob), ("r32", 0), ("cols",)], writes=[("mixA", hp)])

            S1(tiles[0])
            for n in range(N + 2):
                if 1 <= n <= N:
                    S4(tiles[n - 1])
                if n < N:
                    S2a(tiles[n])
                if n + 1 < N:
                    S1(tiles[n + 1])
                if n < N:
                    S2b(tiles[n])
                    S3(tiles[n])
                if 1 <= n <= N:
                    S5(tiles[n - 1])
                if 2 <= n <= N + 1:
                    S6(tiles[n - 2])
                yield 1.0

        PRE_W, POST_W = 62.0, 275.0

        def drive(main_gen, nsteps, side_gens, side_total):
            ready = [0] * len(side_gens)
            alive = [True] * len(side_gens)
            done_w = 0.0
            rr = 0
            for s_ in range(nsteps):
                next(main_gen)
                budget = max(1.5, (side_total - done_w) / max(1, nsteps - s_))
                spent = 0.0
                while spent < budget:
                    cand = [i for i in range(len(side_gens)) if alive[i] and ready[i] <= s_]
                    if not cand:
                        break
                    i = cand[rr % len(cand)]
                    rr += 1
                    try:
                        y = next(side_gens[i])
                    except StopIteration:
                        alive[i] = False
                        continue
                    if isinstance(y, tuple):
                        ready[i] = s_ + y[1]
                    else:
                        spent += y
                done_w += spent
            for _ in main_gen:
                pass
            for g_ in side_gens:
                for _ in g_:
                    pass

        for _ in pre(0):
            pass
        for b in range(NBLK):
            sides, tot = [], 0.0
            if b >= 1:
                sides.append(post(b - 1))
                tot += POST_W
            if b + 1 < NBLK:
                sides.append(pre(b + 1))
                tot += PRE_W
            if b <= 2:
                tot *= 0.8
            nsteps = 16 * (b + 1) + 2
            drive(attn(b), nsteps, [itertools.chain(*sides)] if sides else [], tot)
        for _ in post(NBLK - 1):
            pass

        sch.finalize()
        finals = [(ent[0], ent[1]) for key, ent in sch.dma_sems.items()
                  if isinstance(key, tuple) and key[0] in ("xres_out", "dbg0", "dbg1", "dbg2")]
        with nc.Block() as block:
            @block.tensor
            def _(eng):
                sch.emit_engine("pe", eng)

            @block.scalar
            def _(eng):
                sch.emit_engine("act", eng)

            @block.vector
            def _(eng):
                sch.emit_engine("dve", eng)

            @block.gpsimd
            def _(eng):
                sch.emit_engine("pool", eng)

            @block.sync
            def _(eng):
                sch.emit_engine("sp", eng, final_waits=finals)
    return nc


def _consts():
    bf = ml_dtypes.bfloat16
    cb = np.zeros((NCB, 128, 128), np.float32)
    j = np.arange(128)[:, None]
    s = np.arange(128)[None, :]
    cb[C_IDENT] = np.eye(128)
    cb[C_TRI] = -(j >= s).astype(np.float32)
    cb[C_ONES] = -1.0
    blk = np.zeros((128, 128), np.float32)
    blk[:64, :64] = 1.0 / 64
    blk[64:, 64:] = 1.0 / 64
    cb[C_BLK] = blk
    for g, w in enumerate(POOL_WINDOWS):
        t = s
        diag = ((j <= t) & (j > t - w)).astype(np.float32) / w - (j == t)
        off = ((j - 128) > (t - w)).astype(np.float32) / w
        cnt = np.minimum(t + 1, w).astype(np.float32)
        first = ((j <= t) & (j > t - w)).astype(np.float32) / cnt - (j == t)
        cb[C_POOL + 3 * g + 0] = diag
        cb[C_POOL + 3 * g + 1] = off
        cb[C_POOL + 3 * g + 2] = first
    cb[C_MASK] = np.where(j >= s, NEG, 0.0)
    cbf = np.ascontiguousarray(cb.transpose(1, 0, 2)).astype(bf)
    return cbf, np.eye(128, dtype=np.float32)


_NC_CACHE = {}


def _get_nc(S):
    if S not in _NC_CACHE:
        _NC_CACHE[S] = build_nc(S)
    return _NC_CACHE[S]


def make_in_maps(x, p, emb_ln_g, emb_ln_b, w_in, attn_out_g, w_pool, pool_scale, w_out,
                 ln1_g, ln1_b, w_up, w_down, ln2_g, ln2_b, w_ple, w_ple_gate, ln3_g, ln3_b):
    f = lambda a: np.ascontiguousarray(np.asarray(a, dtype=np.float32))
    B = x.shape[0]
    cbf, identf = _consts()
    lnp = np.stack([f(emb_ln_g), f(emb_ln_b), f(ln1_g)[0], f(ln1_b)[0], f(ln2_g)[0], f(ln2_b)[0],
                    f(ln3_g)[0], f(ln3_b)[0]], axis=0)
    cols = np.concatenate([f(attn_out_g)[0].reshape(4, 128).T, f(pool_scale)[0].reshape(4, 128).T], axis=1)
    shared = {
        "w_in": f(w_in)[0], "w_out": f(w_out)[0], "w_up": f(w_up)[0], "w_down": f(w_down)[0],
        "w_ple": f(w_ple)[0], "w_gate": f(w_ple_gate)[0], "w_pool": f(w_pool)[0],
        "lnp": np.ascontiguousarray(lnp), "cols": np.ascontiguousarray(cols),
        "cbf": cbf, "identf": identf,
    }
    xs = f(x)
    ps_ = f(p)[0]
    in_maps = []
    for c in range(B):
        m = dict(shared)
        m["x"] = xs[c]
        m["p"] = ps_[c]
        in_maps.append(m)
    return in_maps


def kernel(**inputs):
    x = inputs["x"]
    B, S, _ = x.shape
    nc = _get_nc(S)
    in_maps = make_in_maps(**inputs)
    res = run_bass_kernel_spmd(nc, in_maps, core_ids=list(range(B)))
    out = np.stack([np.asarray(r["out"], dtype=np.float32) for r in res.results], axis=0)
    return out
```

```python
import contextlib
import itertools
import numpy as np
import ml_dtypes
import concourse.bass as bass
import concourse.mybir as mybir
from concourse.bass_utils import run_bass_kernel_spmd

F32 = mybir.dt.float32
BF16 = mybir.dt.bfloat16
AF = mybir.ActivationFunctionType
ALU = mybir.AluOpType

D = 1024
DH = 64
DFF = 4096
PLE = 256
TB = 512
TT = 128
ALPHA = float(2.0 ** 0.25)
LN_EPS = 1e-5
RMS_EPS = 1e-6
NEG = -30000.0
POOL_WINDOWS = (2, 4, 8, 16)
NWSLOT = 3
NPIECE = 25

C_IDENT, C_TRI, C_ONES, C_BLK = 0, 1, 2, 3
C_POOL = 4
C_MASK = 16
NCB = 17


class Sched:
    def __init__(self, nc):
        self.nc = nc
        self.ops = []
        self.last_w = {}
        self.readers = {}
        self.eng_sem = {e: nc.alloc_semaphore("sem_" + e) for e in ("pe", "act", "dve", "pool")}
        self.dma_sems = {}

    def add(self, eng, fn, reads=(), writes=(), dma_key=None):
        oid = len(self.ops)
        deps = set()
        for r in reads:
            if r in self.last_w:
                deps.add(self.last_w[r])
        for w in writes:
            if w in self.last_w:
                deps.add(self.last_w[w])
            for rd in self.readers.get(w, ()):
                deps.add(rd)
        op = dict(id=oid, eng=eng, fn=fn, deps=deps, dma_key=dma_key, need_inc=False, cnt=0)
        if dma_key is not None:
            if dma_key not in self.dma_sems:
                self.dma_sems[dma_key] = [self.nc.alloc_semaphore("dsem%d" % len(self.dma_sems)), 0]
            ent = self.dma_sems[dma_key]
            ent[1] += 16
            op["dma_sem"] = ent[0]
            op["dma_val"] = ent[1]
        self.ops.append(op)
        for w in writes:
            self.last_w[w] = oid
            self.readers[w] = []
        for r in reads:
            self.readers.setdefault(r, []).append(oid)
        return oid

    def finalize(self):
        ops = self.ops
        for op in ops:
            for d in op["deps"]:
                B = ops[d]
                if B["dma_key"] is not None:
                    continue
                if B["eng"] == "pe" and op["eng"] == "pe" and op["dma_key"] is None:
                    continue
                B["need_inc"] = True
        cnt = {e: 0 for e in self.eng_sem}
        for op in ops:
            if op["dma_key"] is None and op["need_inc"]:
                cnt[op["eng"]] += 1
                op["cnt"] = cnt[op["eng"]]
        self.by_eng = {}
        for op in ops:
            self.by_eng.setdefault(op["eng"], []).append(op)

    def emit_engine(self, ename, eng, final_waits=()):
        ops = self.ops
        waited = {}
        for op in self.by_eng.get(ename, []):
            need = {}
            for d in op["deps"]:
                B = ops[d]
                if B["dma_key"] is not None:
                    sem, val = B["dma_sem"], B["dma_val"]
                else:
                    if B["eng"] == "pe" and ename == "pe" and op["dma_key"] is None:
                        continue
                    sem, val = self.eng_sem[B["eng"]], B["cnt"]
                k = sem.num
                if k not in need or need[k][1] < val:
                    need[k] = (sem, val)
            for k, (sem, val) in need.items():
                if waited.get(k, 0) >= val:
                    continue
                waited[k] = val
                eng.wait_ge(sem, val)
            ins = op["fn"](eng)
            if op["dma_key"] is not None:
                ins.then_inc(op["dma_sem"], 16)
            elif op["need_inc"]:
                ins.then_inc(self.eng_sem[ename], 1)
        for sem, val in final_waits:
            eng.wait_ge(sem, val)


def build_nc(S, debug=False):
    NBLK = S // TB
    NT = S // TT
    nc = bass.Bass("TRN2", target_bir_lowering=False)

    def din(name, shape, dt=F32):
        return nc.dram_tensor(name, shape, dt, kind="ExternalInput").ap()

    x_d = din("x", [S, D])
    p_d = din("p", [S, PLE])
    w_in_d = din("w_in", [D, 2048])
    w_out_d = din("w_out", [D, D])
    w_up_d = din("w_up", [D, DFF])
    w_down_d = din("w_down", [DFF, D])
    w_ple_d = din("w_ple", [PLE, D])
    w_gate_d = din("w_gate", [D, D])
    w_pool_d = din("w_pool", [4, 128, 128])
    lnp_d = din("lnp", [8, D])
    cols_d = din("cols", [128, 8])
    cbf_d = din("cbf", [128, NCB, 128], BF16)
    identf_d = din("identf", [128, 128])
    out_d = nc.dram_tensor("out", [S, D], F32, kind="ExternalOutput").ap()
    wsc_d = nc.dram_tensor("wsc", [NPIECE, 128, 4096], BF16, kind="Internal").ap()
    if debug:
        dbg_x0 = nc.dram_tensor("dbg_x0", [S, D], F32, kind="ExternalOutput").ap()
        dbg_x1 = nc.dram_tensor("dbg_x1", [S, D], F32, kind="ExternalOutput").ap()
        dbg_x2 = nc.dram_tensor("dbg_x2", [S, D], F32, kind="ExternalOutput").ap()

    es = contextlib.ExitStack()
    with es:
        def sb(name, shape, dt):
            return es.enter_context(nc.sbuf_tensor(name, shape, dt))

        KT = sb("KT", [128, 4, S], BF16)
        V = sb("V", [128, NT, 512], BF16)
        U = sb("U", [128, 5, 512], BF16)
        wsl = sb("wsl", [128, NWSLOT, 4096], BF16)
        gb = sb("gb", [128, 2, D], F32)
        xres = sb("xres", [128, 2, 4, D], F32)
        actT = sb("actT", [128, 8, 512], BF16)
        qT = sb("qT", [128, 2, 4, 512], BF16)
        mixA = sb("mixA", [128, 4, 512], BF16)
        mixP = sb("mixP", [128, 2, 4, 512], BF16)
        e1 = sb("e1", [128, 2, 512], F32)
        sp_r = sb("sp_r", [128, 3, 2, 512], BF16)
        R_r = sb("R_r", [128, 3, 2, 512], BF16)
        W_r = sb("W_r", [128, 2, 2, 512], BF16)
        hT = sb("hT", [128, 16, 512], BF16)
        r32 = sb("r32", [128, 1, 512], F32)
        rst = r32
        dT = sb("dT", [128, 1, 512], BF16)
        sqb = dT[:, 0, :]
        cbf = sb("cbf_s", [128, NCB, 128], BF16)
        identf = sb("identf_s", [128, 128], F32)
        wpool = sb("wpool_s", [128, 4, 128], BF16)
        cols = sb("cols_s", [128, 8], F32)
        st = sb("st", [128, 2, 4, 12], F32)
        mv = sb("mv", [128, 2, 4, 2], F32)
        sm = sb("sm", [128, 2, 4, 4], F32)
        ps = es.enter_context(nc.psum_tensor("ps", [128, 8, 512], F32))

        hflat = hT[:].rearrange("p a b -> p (a b)")
        pbuf = hflat[:, 0:2048].bitcast(F32).rearrange("p (a b) -> p a b", a=4)
        pT = hflat[:, 2048:3072].rearrange("p (a b) -> p a b", a=2)
        egb = hflat[:, 3072:4096].bitcast(F32)
        tmpb = hflat[:, 4096:5120].bitcast(F32)
        HT = lambda s_: ("hT", s_)
        PBUF_R = [HT(0), HT(1), HT(2), HT(3)]
        PT_R = [HT(4), HT(5)]
        EG_R = [HT(6), HT(7)]
        TMP_R = [HT(8), HT(9)]

        sch = Sched(nc)
        add = sch.add

        def bank(k):
            return ("bank", k)

        add("sp", lambda e: e.dma_start(out=cbf[:], in_=cbf_d), writes=[("cbf",)], dma_key="cbf")
        add("sp", lambda e: e.dma_start(out=identf[:], in_=identf_d), writes=[("identf",)], dma_key="identf")
        add("sp", lambda e: e.dma_start(out=cols[:], in_=cols_d), writes=[("cols",)], dma_key="cols")
        add("pool", lambda e: e.dma_start(out=wpool[:], in_=w_pool_d.rearrange("g c d -> c g d")),
            writes=[("wpool",)], dma_key="wpool")
        add("pool", lambda e: e.memset(U[:, 4, :], 0.0), writes=[("U", 4)])

        def piece_src(n):
            if n < 4:
                return w_in_d.rearrange("(kc p) f -> p kc f", p=128)[:, :, n * 512:(n + 1) * 512], 8
            if n < 6:
                h = n - 4
                return w_out_d.rearrange("(kc p) f -> p kc f", p=128)[:, :, h * 512:(h + 1) * 512], 8
            if n < 14:
                j = n - 6
                return w_up_d.rearrange("(kc p) f -> p kc f", p=128)[:, :, j * 512:(j + 1) * 512], 8
            if n < 22:
                m = n - 14
                fh, oh, q = m // 4, (m // 2) % 2, m % 2
                c0 = fh * 16 + q * 8
                return w_down_d.rearrange("(c p) o -> p c o", p=128)[:, c0:c0 + 8, oh * 512:(oh + 1) * 512], 8
            if n < 24:
                h = n - 22
                return w_gate_d.rearrange("(kc p) f -> p kc f", p=128)[:, :, h * 512:(h + 1) * 512], 8
            return w_ple_d.rearrange("(c p) o -> p c o", p=128), 2

        for n in range(NPIECE):
            src, a = piece_src(n)
            if n < 24:
                dst = wsc_d[n].rearrange("p (a b) -> p a b", a=a)
            else:
                dst = wsc_d[n][:, 0:2048].rearrange("p (a b) -> p a b", a=2)
            add("pool", (lambda e, dst=dst, src=src: e.dma_start(out=dst, in_=src)),
                writes=[("wsc", n)], dma_key=("wsc", n))

        PRE_SEQ = [0, 1, 2, 3]
        POST_SEQ = [4, 5]
        for fh in range(2):
            POST_SEQ += [6 + fh * 4 + j for j in range(4)]
            for oh in range(2):
                for tp in range(2):
                    POST_SEQ += [14 + fh * 4 + oh * 2, 14 + fh * 4 + oh * 2 + 1]
        POST_SEQ += [22, 23, 24]
        full_seq = list(PRE_SEQ)
        for b in range(NBLK):
            if b >= 1:
                full_seq += POST_SEQ
            if b + 1 < NBLK:
                full_seq += PRE_SEQ
        POST_SEQ_TAIL = [4, 5]
        for fh in range(2):
            POST_SEQ_TAIL += [6 + fh * 4 + j for j in range(4)]
            for oh in range(2):
                POST_SEQ_TAIL += [14 + fh * 4 + oh * 2, 14 + fh * 4 + oh * 2 + 1]
        POST_SEQ_TAIL += [22, 23, 24]
        full_seq += POST_SEQ_TAIL
        wstate = {"issued": 0, "ptr": 0}

        def issue_upto(g):
            while wstate["issued"] <= min(g, len(full_seq) - 1):
                gi = wstate["issued"]
                n = full_seq[gi]
                s = gi % NWSLOT
                if n < 24:
                    o_ap, i_ap = wsl[:, s, :], wsc_d[n]
                else:
                    o_ap, i_ap = wsl[:, s, 0:2048], wsc_d[n][:, 0:2048]
                add("sp", (lambda e, o_ap=o_ap, i_ap=i_ap: e.dma_start(out=o_ap, in_=i_ap)),
                    reads=[("wsc", n)], writes=[("wslot", s)], dma_key=("wslot", s))
                wstate["issued"] += 1

        def use_pieces(expect):
            k = len(expect)
            assert k <= NWSLOT
            g = wstate["ptr"]
            assert full_seq[g:g + k] == list(expect), (full_seq[g:g + k], expect)
            wstate["ptr"] += k
            issue_upto(g + NWSLOT - 1)
            return [(g + j) % NWSLOT for j in range(k)]

        def wview(s, a):
            return wsl[:, s, :].rearrange("p (a b) -> p a b", a=a)

        sbc = [0]

        def next_bank():
            sbc[0] += 1
            return 6 + sbc[0] % 2

        def load_gb(k):
            for j in range(2):
                add("sp", (lambda e, j=j, k=k: e.dma_start(
                    out=gb[:, j, :], in_=lnp_d[2 * k + j:2 * k + j + 1, :].partition_broadcast(128))),
                    writes=[("gb", j)], dma_key=("gb", j))

        def ln_stats(par, i):
            xr = ("xres", par, i)
            add("dve", lambda e: e.bn_stats(out=st[:, par, i, 0:6], in_=xres[:, par, i, 0:512]),
                reads=[xr], writes=[("st", par, i, 0)])
            add("dve", lambda e: e.bn_stats(out=st[:, par, i, 6:12], in_=xres[:, par, i, 512:1024]),
                reads=[xr], writes=[("st", par, i, 1)])
            add("dve", lambda e: e.bn_aggr(out=mv[:, par, i, :], in_=st[:, par, i, :]),
                reads=[("st", par, i, 0), ("st", par, i, 1)], writes=[("mv", par, i)])

        def transpose_tile(par, i):
            for c in range(8):
                add("pe", (lambda e, c=c: e.transpose(
                    out=ps[:, 6 + c // 4, (c % 4) * 128:(c % 4 + 1) * 128], in_=xres[:, par, i, c * 128:(c + 1) * 128],
                    identity=identf[:])),
                    reads=[("xres", par, i), ("identf",)], writes=[bank(6 + c // 4)])
            for h in range(2):
                add("dve", (lambda e, h=h: e.tensor_copy(
                    out=actT[:, 4 * h:4 * h + 4, i * 128:(i + 1) * 128],
                    in_=ps[:, 6 + h, :].rearrange("p (a b) -> p a b", a=4))),
                    reads=[bank(6 + h)], writes=[("actT", c) for c in range(4 * h, 4 * h + 4)])

        def ln_finish(par, after=None, do_T=True):
            yield ("wait", 1)
            for i in range(4):
                add("act", lambda e, i=i: e.activation(out=sm[:, par, i, 0:1], in_=mv[:, par, i, 1:2], func=AF.Ln,
                                                       bias=LN_EPS, scale=1.0),
                    reads=[("mv", par, i)], writes=[("sm", par, i, 0)])
                add("act", lambda e, i=i: e.activation(out=sm[:, par, i, 1:2], in_=sm[:, par, i, 0:1], func=AF.Exp,
                                                       scale=-0.5),
                    reads=[("sm", par, i, 0)], writes=[("sm", par, i, 1)])
            yield 0.6
            yield ("wait", 1)
            for i in range(4):
                xr = ("xres", par, i)
                xt = xres[:, par, i, :]
                add("dve", lambda e, i=i: e.tensor_scalar(out=sm[:, par, i, 2:3], in0=mv[:, par, i, 0:1],
                                                          scalar1=sm[:, par, i, 1:2], scalar2=-1.0,
                                                          op0=ALU.mult, op1=ALU.mult),
                    reads=[("mv", par, i), ("sm", par, i, 1)], writes=[("sm", par, i, 2)])
                add("dve", lambda e, i=i, xt=xt: e.tensor_scalar(out=xt, in0=xt, scalar1=sm[:, par, i, 1:2],
                                                                 scalar2=sm[:, par, i, 2:3], op0=ALU.mult, op1=ALU.add),
                    reads=[xr, ("sm", par, i, 1), ("sm", par, i, 2)], writes=[xr])
                yield 0.8
                add("dve", lambda e, xt=xt: e.tensor_tensor(out=xt, in0=xt, in1=gb[:, 0, :], op=ALU.mult),
                    reads=[xr, ("gb", 0)], writes=[xr])
                yield 1.2
                add("dve", lambda e, xt=xt: e.tensor_tensor(out=xt, in0=xt, in1=gb[:, 1, :], op=ALU.add),
                    reads=[xr, ("gb", 1)], writes=[xr])
                if after is not None:
                    after(i)
                yield 1.2
                if do_T and i >= 1:
                    transpose_tile(par, i - 1)
                    yield 2.5
            if do_T:
                yield ("wait", 1)
                transpose_tile(par, 3)
                yield 2.5

        def dump(dst, par, i, t0, key):
            add("sp", (lambda e: e.dma_start(out=dst[t0 + i * 128:t0 + (i + 1) * 128, :], in_=xres[:, par, i, :])),
                reads=[("xres", par, i)], dma_key=(key, par, i))

        def to_actT(par):
            prev = None
            for c in range(8):
                bk = next_bank()
                for i in range(4):
                    add("pe", (lambda e, c=c, i=i, bk=bk: e.transpose(
                        out=ps[:, bk, i * 128:(i + 1) * 128], in_=xres[:, par, i, c * 128:(c + 1) * 128],
                        identity=identf[:])),
                        reads=[("xres", par, i), ("identf",)], writes=[bank(bk)])
                if prev is not None:
                    pc, pbk = prev
                    add("dve", (lambda e, pc=pc, pbk=pbk: e.tensor_copy(out=actT[:, pc, :], in_=ps[:, pbk, :])),
                        reads=[bank(pbk)], writes=[("actT", pc)])
                prev = (c, bk)
                yield 1.8
            pc, pbk = prev
            add("dve", (lambda e: e.tensor_copy(out=actT[:, pc, :], in_=ps[:, pbk, :])),
                reads=[bank(pbk)], writes=[("actT", pc)])

        def pre(b):
            par = b % 2
            t0 = b * TB
            load_gb(0)
            for i in range(4):
                add("sp", (lambda e, i=i: e.dma_start(out=xres[:, par, i, :], in_=x_d[t0 + i * 128:t0 + (i + 1) * 128, :])),
                    writes=[("xres", par, i)], dma_key=("xres", par, i))
            for i in range(4):
                ln_stats(par, i)
                yield 1.3
            yield from ln_finish(par, (lambda i: dump(dbg_x0, par, i, t0, "dbg0")) if debug else None)
            for which in range(2):
                s, = use_pieces([which])
                wv = wview(s, 8)
                prev = None
                for hp in range(5):
                    if hp < 4:
                        bk = next_bank()
                        for kc in range(8):
                            add("pe", (lambda e, hp=hp, kc=kc, bk=bk, wv=wv: e.matmul(
                                ps[:, bk, :], lhsT=wv[:, kc, hp * 128:(hp + 1) * 128], rhs=actT[:, kc, :],
                                start=(kc == 0), stop=(kc == 7))),
                                reads=[("wslot", s), ("actT", kc)], writes=[bank(bk)])
                    if prev is not None:
                        php, pbk = prev
                        if which == 0:
                            add("dve", (lambda e, php=php, pbk=pbk: e.tensor_scalar(
                                out=qT[:, par, php, :], in0=ps[:, pbk, :], scalar1=0.125, scalar2=None, op0=ALU.mult)),
                                reads=[bank(pbk)], writes=[("qT", par, php)])
                        else:
                            add("dve", (lambda e, php=php, pbk=pbk: e.tensor_copy(
                                out=KT[:, php, t0:t0 + TB], in_=ps[:, pbk, :])),
                                reads=[bank(pbk)], writes=[("KT", b, php)])
                    prev = (hp, bk) if hp < 4 else None
                    yield 2.0
            for which in range(2):
                s, = use_pieces([2 + which])
                wv = wview(s, 8)
                prev = None
                for i in range(5):
                    if i < 4:
                        bk = next_bank()
                        for kc in range(8):
                            add("pe", (lambda e, i=i, kc=kc, bk=bk, wv=wv: e.matmul(
                                ps[:, bk, :], lhsT=actT[:, kc, i * 128:(i + 1) * 128], rhs=wv[:, kc, :],
                                start=(kc == 0), stop=(kc == 7))),
                                reads=[("wslot", s), ("actT", kc)], writes=[bank(bk)])
                    if prev is not None:
                        pi, pbk = prev
                        T = b * 4 + pi
                        if which == 0:
                            add("dve", (lambda e, T=T, pbk=pbk: e.tensor_copy(out=V[:, T, :], in_=ps[:, pbk, :])),
                                reads=[bank(pbk)], writes=[("V", T)])
                        else:
                            add("dve", (lambda e, T=T, pbk=pbk: e.tensor_copy(out=U[:, T % 5, :], in_=ps[:, pbk, :])),
                                reads=[bank(pbk)], writes=[("U", T % 5)])
                    prev = (i, bk) if i < 4 else None
                    yield 2.0
            for g in range(4):
                bk = next_bank()
                for i in range(4):
                    T = b * 4 + i
                    cur, prv = T % 5, (T - 1) % 5
                    cd = C_POOL + 3 * g + (2 if T == 0 else 0)
                    co = C_POOL + 3 * g + 1
                    add("pe", (lambda e, g=g, i=i, cur=cur, cd=cd, bk=bk: e.matmul(
                        ps[:, bk, i * 128:(i + 1) * 128], lhsT=U[:, cur, g * 128:(g + 1) * 128], rhs=cbf[:, cd, :],
                        start=True, stop=False)),
                        reads=[("U", cur), ("cbf",)], writes=[bank(bk)])
                    add("pe", (lambda e, g=g, i=i, prv=prv, co=co, bk=bk: e.matmul(
                        ps[:, bk, i * 128:(i + 1) * 128], lhsT=U[:, prv, g * 128:(g + 1) * 128], rhs=cbf[:, co, :],
                        start=False, stop=True)),
                        reads=[("U", prv), ("cbf",)], writes=[bank(bk)])
                add("dve", (lambda e, g=g, bk=bk: e.tensor_copy(out=dT[:, 0, :], in_=ps[:, bk, :])),
                    reads=[bank(bk)], writes=[("dT", 0)])
                bk2 = next_bank()
                add("pe", (lambda e, g=g, bk2=bk2: e.matmul(
                    ps[:, bk2, :], lhsT=wpool[:, g, :], rhs=dT[:, 0, :], start=True, stop=True)),
                    reads=[("dT", 0), ("wpool",)], writes=[bank(bk2)])
                add("dve", (lambda e, g=g, bk2=bk2: e.tensor_scalar(
                    out=mixP[:, par, g, :], in0=ps[:, bk2, :], scalar1=cols[:, 4 + g:5 + g], scalar2=None, op0=ALU.mult)),
                    reads=[bank(bk2), ("cols",)], writes=[("mixP", par, g)])
                yield 1.0

        def post(b):
            par = b % 2
            t0 = b * TB
            load_gb(1)
            s0, s1 = use_pieces([4, 5])
            wvs = [wview(s0, 8), wview(s1, 8)]
            for i in range(4):
                for half in range(2):
                    for kc in range(8):
                        lhs = mixA[:, kc, i * 128:(i + 1) * 128] if kc < 4 else mixP[:, par, kc - 4, i * 128:(i + 1) * 128]
                        rd = ("mixA", kc) if kc < 4 else ("mixP", par, kc - 4)
                        add("pe", (lambda e, lhs=lhs, half=half, kc=kc: e.matmul(
                            ps[:, 6 + half, :], lhsT=lhs, rhs=wvs[half][:, kc, :], start=(kc == 0), stop=(kc == 7))),
                            reads=[("wslot", (s0, s1)[half]), rd], writes=[bank(6 + half)])
                    yield 2.0
                add("dve", (lambda e, i=i: e.scalar_tensor_tensor(
                    out=xres[:, par, i, :].rearrange("p (a b) -> p a b", a=2),
                    in0=xres[:, par, i, :].rearrange("p (a b) -> p a b", a=2), scalar=ALPHA,
                    in1=ps[:, 6:8, :], op0=ALU.mult, op1=ALU.add)),
                    reads=[("xres", par, i), bank(6), bank(7)], writes=[("xres", par, i)])
                ln_stats(par, i)
                yield 2.5
            yield from ln_finish(par, (lambda i: dump(dbg_x1, par, i, t0, "dbg1")) if debug else None)
            relu_on_act[0] = (b <= 2) or (b == NBLK - 1)
            load_gb(2)
            for fh in range(2):
                prev = None
                for j in range(4):
                    s, = use_pieces([6 + fh * 4 + j])
                    wv = wview(s, 8)
                    for c in range(4):
                        fcl = 4 * j + c
                        bk = next_bank()
                        for kc in range(8):
                            add("pe", (lambda e, wv=wv, c=c, kc=kc, bk=bk: e.matmul(
                                ps[:, bk, :], lhsT=wv[:, kc, c * 128:(c + 1) * 128], rhs=actT[:, kc, :],
                                start=(kc == 0), stop=(kc == 7))),
                                reads=[("wslot", s), ("actT", kc)], writes=[bank(bk)])
                        if prev is not None:
                            relu2(*prev)
                        prev = (fcl, bk)
                        yield 2.0
                relu2(*prev)
                for oh in (range(2) if b == NBLK - 1 else ()):
                    m0 = 14 + fh * 4 + oh * 2
                    sa, sb_ = use_pieces([m0, m0 + 1])
                    wq = [wview(sa, 8), wview(sb_, 8)]
                    base = 4 * ((fh * 2 + oh) % 2)
                    for q in range(2):
                        for c in range(8):
                            fcl = q * 8 + c
                            for i in range(4):
                                add("pe", (lambda e, wq=wq, q=q, c=c, fcl=fcl, i=i, base=base: e.matmul(
                                    ps[:, base + i, :], lhsT=hT[:, fcl, i * 128:(i + 1) * 128], rhs=wq[q][:, c, :],
                                    start=(fcl == 0), stop=(fcl == 15))),
                                    reads=[("wslot", (sa, sb_)[q]), HT(fcl)], writes=[bank(base + i)])
                            if c % 2 == 1:
                                yield 2.0
                    for i in range(4):
                        xh = xres[:, par, i, oh * 512:(oh + 1) * 512]
                        if fh == 0:
                            add("dve", (lambda e, xh=xh, i=i, base=base: e.scalar_tensor_tensor(
                                out=xh, in0=xh, scalar=ALPHA, in1=ps[:, base + i, :], op0=ALU.mult, op1=ALU.add)),
                                reads=[("xres", par, i), bank(base + i)], writes=[("xres", par, i)])
                        else:
                            add("dve", (lambda e, xh=xh, i=i, base=base: e.tensor_tensor(
                                out=xh, in0=xh, in1=ps[:, base + i, :], op=ALU.add)),
                                reads=[("xres", par, i), bank(base + i)], writes=[("xres", par, i)])
                        if fh == 1 and oh == 1:
                            ln_stats(par, i)
                        yield 0.7
                for oh in (range(2) if b != NBLK - 1 else ()):
                    for tp in range(2):
                        m0 = 14 + fh * 4 + oh * 2
                        sa, sb_ = use_pieces([m0, m0 + 1])
                        wq = [wview(sa, 8), wview(sb_, 8)]
                        for q in range(2):
                            for c in range(8):
                                fcl = q * 8 + c
                                for t in range(2):
                                    i = 2 * tp + t
                                    add("pe", (lambda e, wq=wq, q=q, c=c, fcl=fcl, i=i, t=t: e.matmul(
                                        ps[:, 6 + t, :], lhsT=hT[:, fcl, i * 128:(i + 1) * 128], rhs=wq[q][:, c, :],
                                        start=(fcl == 0), stop=(fcl == 15))),
                                        reads=[("wslot", (sa, sb_)[q]), HT(fcl)], writes=[bank(6 + t)])
                                if c % 4 == 3:
                                    yield 2.0
                        for t in range(2):
                            i = 2 * tp + t
                            xh = xres[:, par, i, oh * 512:(oh + 1) * 512]
                            if fh == 0:
                                add("dve", (lambda e, xh=xh, t=t: e.scalar_tensor_tensor(
                                    out=xh, in0=xh, scalar=ALPHA, in1=ps[:, 6 + t, :], op0=ALU.mult, op1=ALU.add)),
                                    reads=[("xres", par, i), bank(6 + t)], writes=[("xres", par, i)])
                            else:
                                add("dve", (lambda e, xh=xh, t=t: e.tensor_tensor(
                                    out=xh, in0=xh, in1=ps[:, 6 + t, :], op=ALU.add)),
                                    reads=[("xres", par, i), bank(6 + t)], writes=[("xres", par, i)])
                            if fh == 1 and oh == 1:
                                ln_stats(par, i)
                                yield 2.0
                            else:
                                yield 0.7
            yield from ln_finish(par, (lambda i: dump(dbg_x2, par, i, t0, "dbg2")) if debug else None)
            load_gb(3)
            for i in range(4):
                add("sp", (lambda e, i=i: e.dma_start(out=pbuf[:, i, :], in_=p_d[t0 + i * 128:t0 + (i + 1) * 128, :])),
                    writes=PBUF_R, dma_key=("pbuf", i))
            for c in range(2):
                bk = next_bank()
                for i in range(4):
                    add("pe", (lambda e, c=c, i=i, bk=bk: e.transpose(
                        out=ps[:, bk, i * 128:(i + 1) * 128], in_=pbuf[:, i, c * 128:(c + 1) * 128], identity=identf[:])),
                        reads=PBUF_R + [("identf",)], writes=[bank(bk)])
                add("dve", (lambda e, c=c, bk=bk: e.tensor_copy(out=pT[:, c, :], in_=ps[:, bk, :])),
                    reads=[bank(bk)], writes=PT_R)
            yield 1.0
            sg0, sg1, spl = use_pieces([22, 23, 24])
            wg = [wview(sg0, 8), wview(sg1, 8)]
            wpl = wsl[:, spl, 0:2048].rearrange("p (a b) -> p a b", a=2)
            egt = hflat[:, 3072:5120].bitcast(F32)
            egt3 = egt.rearrange("p (a b) -> p a b", a=2)
            EGT_R = EG_R + TMP_R
            for i in range(4):
                gbk = 2 * (i % 2) if b == NBLK - 1 else 6
                pbk = 4 + 2 * (i % 2) if b == NBLK - 1 else 6
                for oh in range(2):
                    for kc in range(8):
                        add("pe", (lambda e, i=i, oh=oh, kc=kc, gbk=gbk: e.matmul(
                            ps[:, gbk + oh, :], lhsT=actT[:, kc, i * 128:(i + 1) * 128], rhs=wg[oh][:, kc, :],
                            start=(kc == 0), stop=(kc == 7))),
                            reads=[("wslot", (sg0, sg1)[oh]), ("actT", kc)], writes=[bank(gbk + oh)])
                    yield 2.0
                yield ("wait", 1)
                add("act", (lambda e, gbk=gbk: e.activation(out=egt3, in_=ps[:, gbk:gbk + 2, :], func=AF.Exp, scale=-1.0)),
                    reads=[bank(gbk), bank(gbk + 1)], writes=EGT_R)
                add("act", (lambda e: e.activation(out=egt, in_=egt, func=AF.Ln, bias=1.0, scale=1.0)),
                    reads=EGT_R, writes=EGT_R)
                add("act", (lambda e: e.activation(out=egt, in_=egt, func=AF.Exp, scale=-1.0)),
                    reads=EGT_R, writes=EGT_R)
                yield 3.3
                yield ("wait", 1)
                for oh in range(2):
                    for c in range(2):
                        add("pe", (lambda e, i=i, oh=oh, c=c, pbk=pbk: e.matmul(
                            ps[:, pbk + oh, :], lhsT=pT[:, c, i * 128:(i + 1) * 128], rhs=wpl[:, c, oh * 512:(oh + 1) * 512],
                            start=(c == 0), stop=(c == 1))),
                            reads=[("wslot", spl)] + PT_R, writes=[bank(pbk + oh)])
                add("dve", (lambda e, pbk=pbk: e.tensor_tensor(out=egt3, in0=ps[:, pbk:pbk + 2, :], in1=egt3, op=ALU.mult)),
                    reads=[bank(pbk), bank(pbk + 1)] + EGT_R, writes=EGT_R)
                xt = xres[:, par, i, :]
                add("dve", (lambda e, xt=xt: e.scalar_tensor_tensor(
                    out=xt, in0=xt, scalar=ALPHA, in1=egt, op0=ALU.mult, op1=ALU.add)),
                    reads=[("xres", par, i)] + EGT_R, writes=[("xres", par, i)])
                ln_stats(par, i)
                yield 3.7
            def store(i):
                add("sp", (lambda e: e.dma_start(out=out_d[t0 + i * 128:t0 + (i + 1) * 128, :], in_=xres[:, par, i, :])),
                    reads=[("xres", par, i)], writes=[("outd", b, i)], dma_key=("xres_out", par, i))
            yield from ln_finish(par, store, do_T=False)

        relu_on_act = [False]

        def relu2(fcl, bk):
            rj = 0
            if relu_on_act[0]:
                add("act", (lambda e: e.activation(out=r32[:, rj, :], in_=ps[:, bk, :], func=AF.Relu)),
                    reads=[bank(bk)], writes=[("r32", rj)])
            else:
                add("dve", (lambda e: e.tensor_scalar(out=r32[:, rj, :], in0=ps[:, bk, :], scalar1=0.0, scalar2=None,
                                                      op0=ALU.max)),
                    reads=[bank(bk)], writes=[("r32", rj)])
            add("dve", (lambda e: e.tensor_tensor(out=hT[:, fcl, :], in0=r32[:, rj, :], in1=r32[:, rj, :], op=ALU.mult)),
                reads=[("r32", rj)], writes=[HT(fcl)])

        zc = [0]

        def attn(b):
            par = b % 2
            tiles = []
            for hp in range(4):
                nk = 4 * b + 4
                for j, kt in enumerate(range(nk - 1, -1, -1)):
                    diag = kt - 4 * b
                    tiles.append(dict(hp=hp, kt=kt, k=j, first=(j == 0), last=(j == nk - 1), diag=diag,
                                      c0=(diag * 128 if diag > 0 else 0), n=len(tiles)))
            N = len(tiles)

            def QK(t, bk0, first_use):
                hp, kt, c0 = t["hp"], t["kt"], t["c0"]
                for hh in range(2):
                    bk = bk0 + hh
                    add("pe", (lambda e, hp=hp, kt=kt, hh=hh, bk=bk, c0=c0: e.matmul(
                        ps[:, bk, c0:512], lhsT=KT[hh * 64:(hh + 1) * 64, hp, kt * 128:(kt + 1) * 128],
                        rhs=qT[hh * 64:(hh + 1) * 64, par, hp, c0:512], start=True, stop=False, skip_group_check=True)),
                        reads=[("KT", kt // 4, hp), ("qT", par, hp)], writes=[bank(bk)])
                if t["diag"] >= 0:
                    for hh in range(2):
                        bk = bk0 + hh
                        add("pe", (lambda e, bk=bk, c0=c0: e.matmul(
                            ps[:, bk, c0:c0 + 128], lhsT=cbf[:, C_IDENT, :], rhs=cbf[:, C_MASK, :],
                            start=False, stop=False, skip_group_check=True)),
                            reads=[("cbf",)], writes=[bank(bk)])

            def S1(t):
                t["zs"] = 0
                QK(t, 0, True)

            def S2a(t):
                c0 = t["c0"]
                add("act", (lambda e, c0=c0: e.activation(
                    out=e1[:, :, c0:512], in_=ps[:, 0:2, c0:512], func=AF.Exp)),
                    reads=[bank(0), bank(1)], writes=[("e",)])

            def S2b(t):
                n, c0 = t["n"], t["c0"]
                sj = n % 3
                if c0 > 0:
                    add("dve", (lambda e, sj=sj, c0=c0: e.memset(sp_r[:, sj, :, 0:c0], 0.0)), writes=[("sp", sj)])
                add("act", (lambda e, sj=sj, c0=c0: e.activation(
                    out=sp_r[:, sj, :, c0:512], in_=e1[:, :, c0:512], func=AF.Ln, bias=1.0, scale=1.0)),
                    reads=[("e",)], writes=[("sp", sj)])

            def r_src(t):
                k, n = t["k"], t["n"]
                if k == 0:
                    return None, None
                if k == 1:
                    return sp_r[:, (n - 1) % 3], ("sp", (n - 1) % 3)
                return R_r[:, k % 3], ("R", k % 3)

            def S3(t):
                k, n = t["k"], t["n"]
                if t["last"] or k == 0:
                    return
                src, sres = r_src(t)
                dj = (k + 1) % 3
                add("dve", (lambda e, src=src, n=n, dj=dj: e.tensor_tensor(
                    out=R_r[:, dj], in0=src, in1=sp_r[:, n % 3], op=ALU.add)),
                    reads=[sres, ("sp", n % 3)], writes=[("R", dj)])

            def S4(t):
                n, c0 = t["n"], t["c0"]
                src, sres = r_src(t)
                QK(t, 2, False)
                for hh in range(2):
                    bk = 2 + hh
                    add("pe", (lambda e, n=n, hh=hh, bk=bk, c0=c0: e.matmul(
                        ps[:, bk, c0:512], lhsT=cbf[:, C_TRI, :], rhs=sp_r[:, n % 3, hh, c0:512],
                        start=False, stop=(src is None), skip_group_check=True)),
                        reads=[("sp", n % 3), ("cbf",)], writes=[bank(bk)])
                    if src is not None:
                        add("pe", (lambda e, hh=hh, bk=bk, c0=c0: e.matmul(
                            ps[:, bk, c0:512], lhsT=cbf[:, C_ONES, :], rhs=src[:, hh, c0:512],
                            start=False, stop=True, skip_group_check=True)),
                            reads=[sres, ("cbf",)], writes=[bank(bk)])

            def S5(t):
                n, c0 = t["n"], t["c0"]
                wj = n % 2
                add("act", (lambda e, wj=wj, c0=c0: e.activation(
                    out=W_r[:, wj, :, c0:512], in_=ps[:, 2:4, c0:512], func=AF.Exp)),
                    reads=[bank(2), bank(3)], writes=[("W", wj)])

            def S6(t):
                n, hp, kt, c0 = t["n"], t["hp"], t["kt"], t["c0"]
                wj = n % 2
                ob = 4 + hp % 2
                for hh in range(2):
                    h = 2 * hp + hh
                    add("pe", (lambda e, kt=kt, h=h, hh=hh, wj=wj, ob=ob, c0=c0, t=t: e.matmul(
                        ps[hh * 64:(hh + 1) * 64, ob, c0:512], lhsT=V[:, kt, h * 64:(h + 1) * 64],
                        rhs=W_r[:, wj, hh, c0:512], start=t["first"], stop=t["last"], skip_group_check=True)),
                        reads=[("V", kt), ("W", wj)], writes=[bank(ob)])
                if t["last"]:
                    epilogue(hp, ob)

            def epilogue(hp, ob):
                mb = 4 + (hp + 1) % 2
                rj = 0
                add("act", (lambda e: e.activation(out=sqb, in_=ps[:, ob, :], func=AF.Square)),
                    reads=[bank(ob)], writes=[("dT", 0)])
                add("pe", (lambda e: e.matmul(ps[:, mb, :], lhsT=cbf[:, C_BLK, :], rhs=sqb, start=True, stop=True)),
                    reads=[("dT", 0), ("cbf",)], writes=[bank(mb)])
                add("act", (lambda e: e.activation(out=rst[:, rj, :], in_=ps[:, mb, :], func=AF.Ln,
                                                   bias=RMS_EPS, scale=1.0)),
                    reads=[bank(mb)], writes=[("r32", 0)])
                add("act", (lambda e: e.activation(out=rst[:, rj, :], in_=rst[:, rj, :], func=AF.Exp, scale=-0.5)),
                    reads=[("r32", 0)], writes=[("r32", 0)])
                add("dve", (lambda e: e.scalar_tensor_tensor(
                    out=mixA[:, hp, :], in0=ps[:, ob, :], scalar=cols[:, hp:hp + 1], in1=rst[:, rj, :],
                    op0=ALU.mult, op1=ALU.mult)),
                    reads=[bank(ob), ("r32", 0), ("cols",)], writes=[("mixA", hp)])

            S1(tiles[0])
            for n in range(N + 2):
                if 1 <= n <= N:
                    S4(tiles[n - 1])
                if n < N:
                    S2a(tiles[n])
                if n + 1 < N:
                    S1(tiles[n + 1])
                if n < N:
                    S2b(tiles[n])
                    S3(tiles[n])
                if 1 <= n <= N:
                    S5(tiles[n - 1])
                if 2 <= n <= N + 1:
                    S6(tiles[n - 2])
                yield 1.0

        PRE_W, POST_W = 62.0, 275.0

        def drive(main_gen, nsteps, side_gens, side_total):
            ready = [0] * len(side_gens)
            alive = [True] * len(side_gens)
            done_w = 0.0
            rr = 0
            for s_ in range(nsteps):
                next(main_gen)
                budget = max(1.5, (side_total - done_w) / max(1, nsteps - s_))
                spent = 0.0
                while spent < budget:
                    cand = [i for i in range(len(side_gens)) if alive[i] and ready[i] <= s_]
                    if not cand:
                        break
                    i = cand[rr % len(cand)]
                    rr += 1
                    try:
                        y = next(side_gens[i])
                    except StopIteration:
                        alive[i] = False
                        continue
                    if isinstance(y, tuple):
                        ready[i] = s_ + y[1]
                    else:
                        spent += y
                done_w += spent
            for _ in main_gen:
                pass
            for g_ in side_gens:
                for _ in g_:
                    pass

        for _ in pre(0):
            pass
        for b in range(NBLK):
            sides, tot = [], 0.0
            if b >= 1:
                sides.append(post(b - 1))
                tot += POST_W
            if b + 1 < NBLK:
                sides.append(pre(b + 1))
                tot += PRE_W
            if b <= 2:
                tot *= 0.8
            nsteps = 16 * (b + 1) + 2
            drive(attn(b), nsteps, [itertools.chain(*sides)] if sides else [], tot)
        for _ in post(NBLK - 1):
            pass

        sch.finalize()
        finals = [(ent[0], ent[1]) for key, ent in sch.dma_sems.items()
                  if isinstance(key, tuple) and key[0] in ("xres_out", "dbg0", "dbg1", "dbg2")]
        with nc.Block() as block:
            @block.tensor
            def _(eng):
                sch.emit_engine("pe", eng)

            @block.scalar
            def _(eng):
                sch.emit_engine("act", eng)

            @block.vector
            def _(eng):
                sch.emit_engine("dve", eng)

            @block.gpsimd
            def _(eng):
                sch.emit_engine("pool", eng)

            @block.sync
            def _(eng):
                sch.emit_engine("sp", eng, final_waits=finals)
    return nc


def _consts():
    bf = ml_dtypes.bfloat16
    cb = np.zeros((NCB, 128, 128), np.float32)
    j = np.arange(128)[:, None]
    s = np.arange(128)[None, :]
    cb[C_IDENT] = np.eye(128)
    cb[C_TRI] = -(j >= s).astype(np.float32)
    cb[C_ONES] = -1.0
    blk = np.zeros((128, 128), np.float32)
    blk[:64, :64] = 1.0 / 64
    blk[64:, 64:] = 1.0 / 64
    cb[C_BLK] = blk
    for g, w in enumerate(POOL_WINDOWS):
        t = s
        diag = ((j <= t) & (j > t - w)).astype(np.float32) / w - (j == t)
        off = ((j - 128) > (t - w)).astype(np.float32) / w
        cnt = np.minimum(t + 1, w).astype(np.float32)
        first = ((j <= t) & (j > t - w)).astype(np.float32) / cnt - (j == t)
        cb[C_POOL + 3 * g + 0] = diag
        cb[C_POOL + 3 * g + 1] = off
        cb[C_POOL + 3 * g + 2] = first
    cb[C_MASK] = np.where(j >= s, NEG, 0.0)
    cbf = np.ascontiguousarray(cb.transpose(1, 0, 2)).astype(bf)
    return cbf, np.eye(128, dtype=np.float32)


_NC_CACHE = {}


def _get_nc(S):
    if S not in _NC_CACHE:
        _NC_CACHE[S] = build_nc(S)
    return _NC_CACHE[S]


def make_in_maps(x, p, emb_ln_g, emb_ln_b, w_in, attn_out_g, w_pool, pool_scale, w_out,
                 ln1_g, ln1_b, w_up, w_down, ln2_g, ln2_b, w_ple, w_ple_gate, ln3_g, ln3_b):
    f = lambda a: np.ascontiguousarray(np.asarray(a, dtype=np.float32))
    B = x.shape[0]
    cbf, identf = _consts()
    lnp = np.stack([f(emb_ln_g), f(emb_ln_b), f(ln1_g)[0], f(ln1_b)[0], f(ln2_g)[0], f(ln2_b)[0],
                    f(ln3_g)[0], f(ln3_b)[0]], axis=0)
    cols = np.concatenate([f(attn_out_g)[0].reshape(4, 128).T, f(pool_scale)[0].reshape(4, 128).T], axis=1)
    shared = {
        "w_in": f(w_in)[0], "w_out": f(w_out)[0], "w_up": f(w_up)[0], "w_down": f(w_down)[0],
        "w_ple": f(w_ple)[0], "w_gate": f(w_ple_gate)[0], "w_pool": f(w_pool)[0],
        "lnp": np.ascontiguousarray(lnp), "cols": np.ascontiguousarray(cols),
        "cbf": cbf, "identf": identf,
    }
    xs = f(x)
    ps_ = f(p)[0]
    in_maps = []
    for c in range(B):
        m = dict(shared)
        m["x"] = xs[c]
        m["p"] = ps_[c]
        in_maps.append(m)
    return in_maps


def kernel(**inputs):
    x = inputs["x"]
    B, S, _ = x.shape
    nc = _get_nc(S)
    in_maps = make_in_maps(**inputs)
    res = run_bass_kernel_spmd(nc, in_maps, core_ids=list(range(B)))
    out = np.stack([np.asarray(r["out"], dtype=np.float32) for r in res.results], axis=0)
    return out
```

```python
import contextlib
import itertools
import numpy as np
import ml_dtypes
import concourse.bass as bass
import concourse.mybir as mybir
from concourse.bass_utils import run_bass_kernel_spmd

F32 = mybir.dt.float32
BF16 = mybir.dt.bfloat16
AF = mybir.ActivationFunctionType
ALU = mybir.AluOpType

D = 1024
DH = 64
DFF = 4096
PLE = 256
TB = 512
TT = 128
ALPHA = float(2.0 ** 0.25)
LN_EPS = 1e-5
RMS_EPS = 1e-6
NEG = -30000.0
POOL_WINDOWS = (2, 4, 8, 16)
NWSLOT = 3
NPIECE = 25

C_IDENT, C_TRI, C_ONES, C_BLK = 0, 1, 2, 3
C_POOL = 4
C_MASK = 16
NCB = 17


class Sched:
    def __init__(self, nc):
        self.nc = nc
        self.ops = []
        self.last_w = {}
        self.readers = {}
        self.eng_sem = {e: nc.alloc_semaphore("sem_" + e) for e in ("pe", "act", "dve", "pool")}
        self.dma_sems = {}

    def add(self, eng, fn, reads=(), writes=(), dma_key=None):
        oid = len(self.ops)
        deps = set()
        for r in reads:
            if r in self.last_w:
                deps.add(self.last_w[r])
        for w in writes:
            if w in self.last_w:
                deps.add(self.last_w[w])
            for rd in self.readers.get(w, ()):
                deps.add(rd)
        op = dict(id=oid, eng=eng, fn=fn, deps=deps, dma_key=dma_key, need_inc=False, cnt=0)
        if dma_key is not None:
            if dma_key not in self.dma_sems:
                self.dma_sems[dma_key] = [self.nc.alloc_semaphore("dsem%d" % len(self.dma_sems)), 0]
            ent = self.dma_sems[dma_key]
            ent[1] += 16
            op["dma_sem"] = ent[0]
            op["dma_val"] = ent[1]
        self.ops.append(op)
        for w in writes:
            self.last_w[w] = oid
            self.readers[w] = []
        for r in reads:
            self.readers.setdefault(r, []).append(oid)
        return oid

    def finalize(self):
        ops = self.ops
        for op in ops:
            for d in op["deps"]:
                B = ops[d]
                if B["dma_key"] is not None:
                    continue
                if B["eng"] == "pe" and op["eng"] == "pe" and op["dma_key"] is None:
                    continue
                B["need_inc"] = True
        cnt = {e: 0 for e in self.eng_sem}
        for op in ops:
            if op["dma_key"] is None and op["need_inc"]:
                cnt[op["eng"]] += 1
                op["cnt"] = cnt[op["eng"]]
        self.by_eng = {}
        for op in ops:
            self.by_eng.setdefault(op["eng"], []).append(op)

    def emit_engine(self, ename, eng, final_waits=()):
        ops = self.ops
        waited = {}
        for op in self.by_eng.get(ename, []):
            need = {}
            for d in op["deps"]:
                B = ops[d]
                if B["dma_key"] is not None:
                    sem, val = B["dma_sem"], B["dma_val"]
                else:
                    if B["eng"] == "pe" and ename == "pe" and op["dma_key"] is None:
                        continue
                    sem, val = self.eng_sem[B["eng"]], B["cnt"]
                k = sem.num
                if k not in need or need[k][1] < val:
                    need[k] = (sem, val)
            for k, (sem, val) in need.items():
                if waited.get(k, 0) >= val:
                    continue
                waited[k] = val
                eng.wait_ge(sem, val)
            ins = op["fn"](eng)
            if op["dma_key"] is not None:
                ins.then_inc(op["dma_sem"], 16)
            elif op["need_inc"]:
                ins.then_inc(self.eng_sem[ename], 1)
        for sem, val in final_waits:
            eng.wait_ge(sem, val)


def build_nc(S, debug=False):
    NBLK = S // TB
    NT = S // TT
    nc = bass.Bass("TRN2", target_bir_lowering=False)

    def din(name, shape, dt=F32):
        return nc.dram_tensor(name, shape, dt, kind="ExternalInput").ap()

    x_d = din("x", [S, D])
    p_d = din("p", [S, PLE])
    w_in_d = din("w_in", [D, 2048])
    w_out_d = din("w_out", [D, D])
    w_up_d = din("w_up", [D, DFF])
    w_down_d = din("w_down", [DFF, D])
    w_ple_d = din("w_ple", [PLE, D])
    w_gate_d = din("w_gate", [D, D])
    w_pool_d = din("w_pool", [4, 128, 128])
    lnp_d = din("lnp", [8, D])
    cols_d = din("cols", [128, 8])
    cbf_d = din("cbf", [128, NCB, 128], BF16)
    identf_d = din("identf", [128, 128])
    out_d = nc.dram_tensor("out", [S, D], F32, kind="ExternalOutput").ap()
    wsc_d = nc.dram_tensor("wsc", [NPIECE, 128, 4096], BF16, kind="Internal").ap()
    if debug:
        dbg_x0 = nc.dram_tensor("dbg_x0", [S, D], F32, kind="ExternalOutput").ap()
        dbg_x1 = nc.dram_tensor("dbg_x1", [S, D], F32, kind="ExternalOutput").ap()
        dbg_x2 = nc.dram_tensor("dbg_x2", [S, D], F32, kind="ExternalOutput").ap()

    es = contextlib.ExitStack()
    with es:
        def sb(name, shape, dt):
            return es.enter_context(nc.sbuf_tensor(name, shape, dt))

        KT = sb("KT", [128, 4, S], BF16)
        V = sb("V", [128, NT, 512], BF16)
        U = sb("U", [128, 5, 512], BF16)
        wsl = sb("wsl", [128, NWSLOT, 4096], BF16)
        gb = sb("gb", [128, 2, D], F32)
        xres = sb("xres", [128, 2, 4, D], F32)
        actT = sb("actT", [128, 8, 512], BF16)
        qT = sb("qT", [128, 2, 4, 512], BF16)
        mixA = sb("mixA", [128, 4, 512], BF16)
        mixP = sb("mixP", [128, 2, 4, 512], BF16)
        e1 = sb("e1", [128, 2, 512], F32)
        sp_r = sb("sp_r", [128, 3, 2, 512], BF16)
        R_r = sb("R_r", [128, 3, 2, 512], BF16)
        W_r = sb("W_r", [128, 2, 2, 512], BF16)
        hT = sb("hT", [128, 16, 512], BF16)
        r32 = sb("r32", [128, 1, 512], F32)
        rst = r32
        dT = sb("dT", [128, 1, 512], BF16)
        sqb = dT[:, 0, :]
        cbf = sb("cbf_s", [128, NCB, 128], BF16)
        identf = sb("identf_s", [128, 128], F32)
        wpool = sb("wpool_s", [128, 4, 128], BF16)
        cols = sb("cols_s", [128, 8], F32)
        st = sb("st", [128, 2, 4, 12], F32)
        mv = sb("mv", [128, 2, 4, 2], F32)
        sm = sb("sm", [128, 2, 4, 4], F32)
        ps = es.enter_context(nc.psum_tensor("ps", [128, 8, 512], F32))

        hflat = hT[:].rearrange("p a b -> p (a b)")
        pbuf = hflat[:, 0:2048].bitcast(F32).rearrange("p (a b) -> p a b", a=4)
        pT = hflat[:, 2048:3072].rearrange("p (a b) -> p a b", a=2)
        egb = hflat[:, 3072:4096].bitcast(F32)
        tmpb = hflat[:, 4096:5120].bitcast(F32)
        HT = lambda s_: ("hT", s_)
        PBUF_R = [HT(0), HT(1), HT(2), HT(3)]
        PT_R = [HT(4), HT(5)]
        EG_R = [HT(6), HT(7)]
        TMP_R = [HT(8), HT(9)]

        sch = Sched(nc)
        add = sch.add

        def bank(k):
            return ("bank", k)

        add("sp", lambda e: e.dma_start(out=cbf[:], in_=cbf_d), writes=[("cbf",)], dma_key="cbf")
        add("sp", lambda e: e.dma_start(out=identf[:], in_=identf_d), writes=[("identf",)], dma_key="identf")
        add("sp", lambda e: e.dma_start(out=cols[:], in_=cols_d), writes=[("cols",)], dma_key="cols")
        add("pool", lambda e: e.dma_start(out=wpool[:], in_=w_pool_d.rearrange("g c d -> c g d")),
            writes=[("wpool",)], dma_key="wpool")
        add("pool", lambda e: e.memset(U[:, 4, :], 0.0), writes=[("U", 4)])

        def piece_src(n):
            if n < 4:
                return w_in_d.rearrange("(kc p) f -> p kc f", p=128)[:, :, n * 512:(n + 1) * 512], 8
            if n < 6:
                h = n - 4
                return w_out_d.rearrange("(kc p) f -> p kc f", p=128)[:, :, h * 512:(h + 1) * 512], 8
            if n < 14:
                j = n - 6
                return w_up_d.rearrange("(kc p) f -> p kc f", p=128)[:, :, j * 512:(j + 1) * 512], 8
            if n < 22:
                m = n - 14
                fh, oh, q = m // 4, (m // 2) % 2, m % 2
                c0 = fh * 16 + q * 8
                return w_down_d.rearrange("(c p) o -> p c o", p=128)[:, c0:c0 + 8, oh * 512:(oh + 1) * 512], 8
            if n < 24:
                h = n - 22
                return w_gate_d.rearrange("(kc p) f -> p kc f", p=128)[:, :, h * 512:(h + 1) * 512], 8
            return w_ple_d.rearrange("(c p) o -> p c o", p=128), 2

        for n in range(NPIECE):
            src, a = piece_src(n)
            if n < 24:
                dst = wsc_d[n].rearrange("p (a b) -> p a b", a=a)
            else:
                dst = wsc_d[n][:, 0:2048].rearrange("p (a b) -> p a b", a=2)
            add("pool", (lambda e, dst=dst, src=src: e.dma_start(out=dst, in_=src)),
                writes=[("wsc", n)], dma_key=("wsc", n))

        PRE_SEQ = [0, 1, 2, 3]
        POST_SEQ = [4, 5]
        for fh in range(2):
            POST_SEQ += [6 + fh * 4 + j for j in range(4)]
            for oh in range(2):
                for tp in range(2):
                    POST_SEQ += [14 + fh * 4 + oh * 2, 14 + fh * 4 + oh * 2 + 1]
        POST_SEQ += [22, 23, 24]
        full_seq = list(PRE_SEQ)
        for b in range(NBLK):
            if b >= 1:
                full_seq += POST_SEQ
            if b + 1 < NBLK:
                full_seq += PRE_SEQ
        POST_SEQ_TAIL = [4, 5]
        for fh in range(2):
            POST_SEQ_TAIL += [6 + fh * 4 + j for j in range(4)]
            for oh in range(2):
                POST_SEQ_TAIL += [14 + fh * 4 + oh * 2, 14 + fh * 4 + oh * 2 + 1]
        POST_SEQ_TAIL += [22, 23, 24]
        full_seq += POST_SEQ_TAIL
        wstate = {"issued": 0, "ptr": 0}

        def issue_upto(g):
            while wstate["issued"] <= min(g, len(full_seq) - 1):
                gi = wstate["issued"]
                n = full_seq[gi]
                s = gi % NWSLOT
                if n < 24:
                    o_ap, i_ap = wsl[:, s, :], wsc_d[n]
                else:
                    o_ap, i_ap = wsl[:, s, 0:2048], wsc_d[n][:, 0:2048]
                add("sp", (lambda e, o_ap=o_ap, i_ap=i_ap: e.dma_start(out=o_ap, in_=i_ap)),
                    reads=[("wsc", n)], writes=[("wslot", s)], dma_key=("wslot", s))
                wstate["issued"] += 1

        def use_pieces(expect):
            k = len(expect)
            assert k <= NWSLOT
            g = wstate["ptr"]
            assert full_seq[g:g + k] == list(expect), (full_seq[g:g + k], expect)
            wstate["ptr"] += k
            issue_upto(g + NWSLOT - 1)
            return [(g + j) % NWSLOT for j in range(k)]

        def wview(s, a):
            return wsl[:, s, :].rearrange("p (a b) -> p a b", a=a)

        sbc = [0]

        def next_bank():
            sbc[0] += 1
            return 6 + sbc[0] % 2

        def load_gb(k):
            for j in range(2):
                add("sp", (lambda e, j=j, k=k: e.dma_start(
                    out=gb[:, j, :], in_=lnp_d[2 * k + j:2 * k + j + 1, :].partition_broadcast(128))),
                    writes=[("gb", j)], dma_key=("gb", j))

        def ln_stats(par, i):
            xr = ("xres", par, i)
            add("dve", lambda e: e.bn_stats(out=st[:, par, i, 0:6], in_=xres[:, par, i, 0:512]),
                reads=[xr], writes=[("st", par, i, 0)])
            add("dve", lambda e: e.bn_stats(out=st[:, par, i, 6:12], in_=xres[:, par, i, 512:1024]),
                reads=[xr], writes=[("st", par, i, 1)])
            add("dve", lambda e: e.bn_aggr(out=mv[:, par, i, :], in_=st[:, par, i, :]),
                reads=[("st", par, i, 0), ("st", par, i, 1)], writes=[("mv", par, i)])

        def transpose_tile(par, i):
            for c in range(8):
                add("pe", (lambda e, c=c: e.transpose(
                    out=ps[:, 6 + c // 4, (c % 4) * 128:(c % 4 + 1) * 128], in_=xres[:, par, i, c * 128:(c + 1) * 128],
                    identity=identf[:])),
                    reads=[("xres", par, i), ("identf",)], writes=[bank(6 + c // 4)])
            for h in range(2):
                add("dve", (lambda e, h=h: e.tensor_copy(
                    out=actT[:, 4 * h:4 * h + 4, i * 128:(i + 1) * 128],
                    in_=ps[:, 6 + h, :].rearrange("p (a b) -> p a b", a=4))),
                    reads=[bank(6 + h)], writes=[("actT", c) for c in range(4 * h, 4 * h + 4)])

        def ln_finish(par, after=None, do_T=True):
            yield ("wait", 1)
            for i in range(4):
                add("act", lambda e, i=i: e.activation(out=sm[:, par, i, 0:1], in_=mv[:, par, i, 1:2], func=AF.Ln,
                                                       bias=LN_EPS, scale=1.0),
                    reads=[("mv", par, i)], writes=[("sm", par, i, 0)])
                add("act", lambda e, i=i: e.activation(out=sm[:, par, i, 1:2], in_=sm[:, par, i, 0:1], func=AF.Exp,
                                                       scale=-0.5),
                    reads=[("sm", par, i, 0)], writes=[("sm", par, i, 1)])
            yield 0.6
            yield ("wait", 1)
            for i in range(4):
                xr = ("xres", par, i)
                xt = xres[:, par, i, :]
                add("dve", lambda e, i=i: e.tensor_scalar(out=sm[:, par, i, 2:3], in0=mv[:, par, i, 0:1],
                                                          scalar1=sm[:, par, i, 1:2], scalar2=-1.0,
                                                          op0=ALU.mult, op1=ALU.mult),
                    reads=[("mv", par, i), ("sm", par, i, 1)], writes=[("sm", par, i, 2)])
                add("dve", lambda e, i=i, xt=xt: e.tensor_scalar(out=xt, in0=xt, scalar1=sm[:, par, i, 1:2],
                                                                 scalar2=sm[:, par, i, 2:3], op0=ALU.mult, op1=ALU.add),
                    reads=[xr, ("sm", par, i, 1), ("sm", par, i, 2)], writes=[xr])
                yield 0.8
                add("dve", lambda e, xt=xt: e.tensor_tensor(out=xt, in0=xt, in1=gb[:, 0, :], op=ALU.mult),
                    reads=[xr, ("gb", 0)], writes=[xr])
                yield 1.2
                add("dve", lambda e, xt=xt: e.tensor_tensor(out=xt, in0=xt, in1=gb[:, 1, :], op=ALU.add),
                    reads=[xr, ("gb", 1)], writes=[xr])
                if after is not None:
                    after(i)
                yield 1.2
                if do_T and i >= 1:
                    transpose_tile(par, i - 1)
                    yield 2.5
            if do_T:
                yield ("wait", 1)
                transpose_tile(par, 3)
                yield 2.5

        def dump(dst, par, i, t0, key):
            add("sp", (lambda e: e.dma_start(out=dst[t0 + i * 128:t0 + (i + 1) * 128, :], in_=xres[:, par, i, :])),
                reads=[("xres", par, i)], dma_key=(key, par, i))

        def to_actT(par):
            prev = None
            for c in range(8):
                bk = next_bank()
                for i in range(4):
                    add("pe", (lambda e, c=c, i=i, bk=bk: e.transpose(
                        out=ps[:, bk, i * 128:(i + 1) * 128], in_=xres[:, par, i, c * 128:(c + 1) * 128],
                        identity=identf[:])),
                        reads=[("xres", par, i), ("identf",)], writes=[bank(bk)])
                if prev is not None:
                    pc, pbk = prev
                    add("dve", (lambda e, pc=pc, pbk=pbk: e.tensor_copy(out=actT[:, pc, :], in_=ps[:, pbk, :])),
                        reads=[bank(pbk)], writes=[("actT", pc)])
                prev = (c, bk)
                yield 1.8
            pc, pbk = prev
            add("dve", (lambda e: e.tensor_copy(out=actT[:, pc, :], in_=ps[:, pbk, :])),
                reads=[bank(pbk)], writes=[("actT", pc)])

        def pre(b):
            par = b % 2
            t0 = b * TB
            load_gb(0)
            for i in range(4):
                add("sp", (lambda e, i=i: e.dma_start(out=xres[:, par, i, :], in_=x_d[t0 + i * 128:t0 + (i + 1) * 128, :])),
                    writes=[("xres", par, i)], dma_key=("xres", par, i))
            for i in range(4):
                ln_stats(par, i)
                yield 1.3
            yield from ln_finish(par, (lambda i: dump(dbg_x0, par, i, t0, "dbg0")) if debug else None)
            for which in range(2):
                s, = use_pieces([which])
                wv = wview(s, 8)
                prev = None
                for hp in range(5):
                    if hp < 4:
                        bk = next_bank()
                        for kc in range(8):
                            add("pe", (lambda e, hp=hp, kc=kc, bk=bk, wv=wv: e.matmul(
                                ps[:, bk, :], lhsT=wv[:, kc, hp * 128:(hp + 1) * 128], rhs=actT[:, kc, :],
                                start=(kc == 0), stop=(kc == 7))),
                                reads=[("wslot", s), ("actT", kc)], writes=[bank(bk)])
                    if prev is not None:
                        php, pbk = prev
                        if which == 0:
                            add("dve", (lambda e, php=php, pbk=pbk: e.tensor_scalar(
                                out=qT[:, par, php, :], in0=ps[:, pbk, :], scalar1=0.125, scalar2=None, op0=ALU.mult)),
                                reads=[bank(pbk)], writes=[("qT", par, php)])
                        else:
                            add("dve", (lambda e, php=php, pbk=pbk: e.tensor_copy(
                                out=KT[:, php, t0:t0 + TB], in_=ps[:, pbk, :])),
                                reads=[bank(pbk)], writes=[("KT", b, php)])
                    prev = (hp, bk) if hp < 4 else None
                    yield 2.0
            for which in range(2):
                s, = use_pieces([2 + which])
                wv = wview(s, 8)
                prev = None
                for i in range(5):
                    if i < 4:
                        bk = next_bank()
                        for kc in range(8):
                            add("pe", (lambda e, i=i, kc=kc, bk=bk, wv=wv: e.matmul(
                                ps[:, bk, :], lhsT=actT[:, kc, i * 128:(i + 1) * 128], rhs=wv[:, kc, :],
                                start=(kc == 0), stop=(kc == 7))),
                                reads=[("wslot", s), ("actT", kc)], writes=[bank(bk)])
                    if prev is not None:
                        pi, pbk = prev
                        T = b * 4 + pi
                        if which == 0:
                            add("dve", (lambda e, T=T, pbk=pbk: e.tensor_copy(out=V[:, T, :], in_=ps[:, pbk, :])),
                                reads=[bank(pbk)], writes=[("V", T)])
                        else:
                            add("dve", (lambda e, T=T, pbk=pbk: e.tensor_copy(out=U[:, T % 5, :], in_=ps[:, pbk, :])),
                                reads=[bank(pbk)], writes=[("U", T % 5)])
                    prev = (i, bk) if i < 4 else None
                    yield 2.0
            for g in range(4):
                bk = next_bank()
                for i in range(4):
                    T = b * 4 + i
                    cur, prv = T % 5, (T - 1) % 5
                    cd = C_POOL + 3 * g + (2 if T == 0 else 0)
                    co = C_POOL + 3 * g + 1
                    add("pe", (lambda e, g=g, i=i, cur=cur, cd=cd, bk=bk: e.matmul(
                        ps[:, bk, i * 128:(i + 1) * 128], lhsT=U[:, cur, g * 128:(g + 1) * 128], rhs=cbf[:, cd, :],
                        start=True, stop=False)),
                        reads=[("U", cur), ("cbf",)], writes=[bank(bk)])
                    add("pe", (lambda e, g=g, i=i, prv=prv, co=co, bk=bk: e.matmul(
                        ps[:, bk, i * 128:(i + 1) * 128], lhsT=U[:, prv, g * 128:(g + 1) * 128], rhs=cbf[:, co, :],
                        start=False, stop=True)),
                        reads=[("U", prv), ("cbf",)], writes=[bank(bk)])
                add("dve", (lambda e, g=g, bk=bk: e.tensor_copy(out=dT[:, 0, :], in_=ps[:, bk, :])),
                    reads=[bank(bk)], writes=[("dT", 0)])
                bk2 = next_bank()
                add("pe", (lambda e, g=g, bk2=bk2: e.matmul(
                    ps[:, bk2, :], lhsT=wpool[:, g, :], rhs=dT[:, 0, :], start=True, stop=True)),
                    reads=[("dT", 0), ("wpool",)], writes=[bank(bk2)])
                add("dve", (lambda e, g=g, bk2=bk2: e.tensor_scalar(
                    out=mixP[:, par, g, :], in0=ps[:, bk2, :], scalar1=cols[:, 4 + g:5 + g], scalar2=None, op0=ALU.mult)),
                    reads=[bank(bk2), ("cols",)], writes=[("mixP", par, g)])
                yield 1.0

        def post(b):
            par = b % 2
            t0 = b * TB
            load_gb(1)
            s0, s1 = use_pieces([4, 5])
            wvs = [wview(s0, 8), wview(s1, 8)]
            for i in range(4):
                eb = 2 * i if b == NBLK - 1 else 6
                for half in range(2):
                    for kc in range(8):
                        lhs = mixA[:, kc, i * 128:(i + 1) * 128] if kc < 4 else mixP[:, par, kc - 4, i * 128:(i + 1) * 128]
                        rd = ("mixA", kc) if kc < 4 else ("mixP", par, kc - 4)
                        add("pe", (lambda e, lhs=lhs, half=half, kc=kc, eb=eb: e.matmul(
                            ps[:, eb + half, :], lhsT=lhs, rhs=wvs[half][:, kc, :], start=(kc == 0), stop=(kc == 7))),
                            reads=[("wslot", (s0, s1)[half]), rd], writes=[bank(eb + half)])
                    yield 2.0
                add("dve", (lambda e, i=i, eb=eb: e.scalar_tensor_tensor(
                    out=xres[:, par, i, :].rearrange("p (a b) -> p a b", a=2),
                    in0=xres[:, par, i, :].rearrange("p (a b) -> p a b", a=2), scalar=ALPHA,
                    in1=ps[:, eb:eb + 2, :], op0=ALU.mult, op1=ALU.add)),
                    reads=[("xres", par, i), bank(eb), bank(eb + 1)], writes=[("xres", par, i)])
                ln_stats(par, i)
                yield 2.5
            yield from ln_finish(par, (lambda i: dump(dbg_x1, par, i, t0, "dbg1")) if debug else None)
            relu_on_act[0] = (b <= 2) or (b == NBLK - 1)
            load_gb(2)
            for fh in range(2):
                prev = None
                for j in range(4):
                    s, = use_pieces([6 + fh * 4 + j])
                    wv = wview(s, 8)
                    for c in range(4):
                        fcl = 4 * j + c
                        bk = next_bank()
                        for kc in range(8):
                            add("pe", (lambda e, wv=wv, c=c, kc=kc, bk=bk: e.matmul(
                                ps[:, bk, :], lhsT=wv[:, kc, c * 128:(c + 1) * 128], rhs=actT[:, kc, :],
                                start=(kc == 0), stop=(kc == 7))),
                                reads=[("wslot", s), ("actT", kc)], writes=[bank(bk)])
                        if prev is not None:
                            relu2(*prev)
                        prev = (fcl, bk)
                        yield 2.0
                relu2(*prev)
                for oh in (range(2) if b == NBLK - 1 else ()):
                    m0 = 14 + fh * 4 + oh * 2
                    sa, sb_ = use_pieces([m0, m0 + 1])
                    wq = [wview(sa, 8), wview(sb_, 8)]
                    base = 4 * ((fh * 2 + oh) % 2)
                    for q in range(2):
                        for c in range(8):
                            fcl = q * 8 + c
                            for i in range(4):
                                add("pe", (lambda e, wq=wq, q=q, c=c, fcl=fcl, i=i, base=base: e.matmul(
                                    ps[:, base + i, :], lhsT=hT[:, fcl, i * 128:(i + 1) * 128], rhs=wq[q][:, c, :],
                                    start=(fcl == 0), stop=(fcl == 15))),
                                    reads=[("wslot", (sa, sb_)[q]), HT(fcl)], writes=[bank(base + i)])
                            if c % 2 == 1:
                                yield 2.0
                    for i in range(4):
                        xh = xres[:, par, i, oh * 512:(oh + 1) * 512]
                        if fh == 0:
                            add("dve", (lambda e, xh=xh, i=i, base=base: e.scalar_tensor_tensor(
                                out=xh, in0=xh, scalar=ALPHA, in1=ps[:, base + i, :], op0=ALU.mult, op1=ALU.add)),
                                reads=[("xres", par, i), bank(base + i)], writes=[("xres", par, i)])
                        else:
                            add("dve", (lambda e, xh=xh, i=i, base=base: e.tensor_tensor(
                                out=xh, in0=xh, in1=ps[:, base + i, :], op=ALU.add)),
                                reads=[("xres", par, i), bank(base + i)], writes=[("xres", par, i)])
                        if fh == 1 and oh == 1:
                            ln_stats(par, i)
                        yield 0.7
                for oh in (range(2) if b != NBLK - 1 else ()):
                    for tp in range(2):
                        m0 = 14 + fh * 4 + oh * 2
                        sa, sb_ = use_pieces([m0, m0 + 1])
                        wq = [wview(sa, 8), wview(sb_, 8)]
                        for q in range(2):
                            for c in range(8):
                                fcl = q * 8 + c
                                for t in range(2):
                                    i = 2 * tp + t
                                    add("pe", (lambda e, wq=wq, q=q, c=c, fcl=fcl, i=i, t=t: e.matmul(
                                        ps[:, 6 + t, :], lhsT=hT[:, fcl, i * 128:(i + 1) * 128], rhs=wq[q][:, c, :],
                                        start=(fcl == 0), stop=(fcl == 15))),
                                        reads=[("wslot", (sa, sb_)[q]), HT(fcl)], writes=[bank(6 + t)])
                                if c % 4 == 3:
                                    yield 2.0
                        for t in range(2):
                            i = 2 * tp + t
                            xh = xres[:, par, i, oh * 512:(oh + 1) * 512]
                            if fh == 0:
                                add("dve", (lambda e, xh=xh, t=t: e.scalar_tensor_tensor(
                                    out=xh, in0=xh, scalar=ALPHA, in1=ps[:, 6 + t, :], op0=ALU.mult, op1=ALU.add)),
                                    reads=[("xres", par, i), bank(6 + t)], writes=[("xres", par, i)])
                            else:
                                add("dve", (lambda e, xh=xh, t=t: e.tensor_tensor(
                                    out=xh, in0=xh, in1=ps[:, 6 + t, :], op=ALU.add)),
                                    reads=[("xres", par, i), bank(6 + t)], writes=[("xres", par, i)])
                            if fh == 1 and oh == 1:
                                ln_stats(par, i)
                                yield 2.0
                            else:
                                yield 0.7
            yield from ln_finish(par, (lambda i: dump(dbg_x2, par, i, t0, "dbg2")) if debug else None)
            load_gb(3)
            for i in range(4):
                add("sp", (lambda e, i=i: e.dma_start(out=pbuf[:, i, :], in_=p_d[t0 + i * 128:t0 + (i + 1) * 128, :])),
                    writes=PBUF_R, dma_key=("pbuf", i))
            for c in range(2):
                bk = next_bank()
                for i in range(4):
                    add("pe", (lambda e, c=c, i=i, bk=bk: e.transpose(
                        out=ps[:, bk, i * 128:(i + 1) * 128], in_=pbuf[:, i, c * 128:(c + 1) * 128], identity=identf[:])),
                        reads=PBUF_R + [("identf",)], writes=[bank(bk)])
                add("dve", (lambda e, c=c, bk=bk: e.tensor_copy(out=pT[:, c, :], in_=ps[:, bk, :])),
                    reads=[bank(bk)], writes=PT_R)
            yield 1.0
            sg0, sg1, spl = use_pieces([22, 23, 24])
            wg = [wview(sg0, 8), wview(sg1, 8)]
            wpl = wsl[:, spl, 0:2048].rearrange("p (a b) -> p a b", a=2)
            egt = hflat[:, 3072:5120].bitcast(F32)
            egt3 = egt.rearrange("p (a b) -> p a b", a=2)
            EGT_R = EG_R + TMP_R
            for i in range(4):
                gbk = 2 * (i % 2) if b == NBLK - 1 else 6
                pbk = 4 + 2 * (i % 2) if b == NBLK - 1 else 6
                for oh in range(2):
                    for kc in range(8):
                        add("pe", (lambda e, i=i, oh=oh, kc=kc, gbk=gbk: e.matmul(
                            ps[:, gbk + oh, :], lhsT=actT[:, kc, i * 128:(i + 1) * 128], rhs=wg[oh][:, kc, :],
                            start=(kc == 0), stop=(kc == 7))),
                            reads=[("wslot", (sg0, sg1)[oh]), ("actT", kc)], writes=[bank(gbk + oh)])
                    yield 2.0
                yield ("wait", 1)
                add("act", (lambda e, gbk=gbk: e.activation(out=egt3, in_=ps[:, gbk:gbk + 2, :], func=AF.Exp, scale=-1.0)),
                    reads=[bank(gbk), bank(gbk + 1)], writes=EGT_R)
                add("act", (lambda e: e.activation(out=egt, in_=egt, func=AF.Ln, bias=1.0, scale=1.0)),
                    reads=EGT_R, writes=EGT_R)
                add("act", (lambda e: e.activation(out=egt, in_=egt, func=AF.Exp, scale=-1.0)),
                    reads=EGT_R, writes=EGT_R)
                yield 3.3
                yield ("wait", 1)
                for oh in range(2):
                    for c in range(2):
                        add("pe", (lambda e, i=i, oh=oh, c=c, pbk=pbk: e.matmul(
                            ps[:, pbk + oh, :], lhsT=pT[:, c, i * 128:(i + 1) * 128], rhs=wpl[:, c, oh * 512:(oh + 1) * 512],
                            start=(c == 0), stop=(c == 1))),
                            reads=[("wslot", spl)] + PT_R, writes=[bank(pbk + oh)])
                add("dve", (lambda e, pbk=pbk: e.tensor_tensor(out=egt3, in0=ps[:, pbk:pbk + 2, :], in1=egt3, op=ALU.mult)),
                    reads=[bank(pbk), bank(pbk + 1)] + EGT_R, writes=EGT_R)
                xt = xres[:, par, i, :]
                add("dve", (lambda e, xt=xt: e.scalar_tensor_tensor(
                    out=xt, in0=xt, scalar=ALPHA, in1=egt, op0=ALU.mult, op1=ALU.add)),
                    reads=[("xres", par, i)] + EGT_R, writes=[("xres", par, i)])
                ln_stats(par, i)
                yield 3.7
            def store(i):
                add("sp", (lambda e: e.dma_start(out=out_d[t0 + i * 128:t0 + (i + 1) * 128, :], in_=xres[:, par, i, :])),
                    reads=[("xres", par, i)], writes=[("outd", b, i)], dma_key=("xres_out", par, i))
            yield from ln_finish(par, store, do_T=False)

        relu_on_act = [False]

        def relu2(fcl, bk):
            rj = 0
            if relu_on_act[0]:
                add("act", (lambda e: e.activation(out=r32[:, rj, :], in_=ps[:, bk, :], func=AF.Relu)),
                    reads=[bank(bk)], writes=[("r32", rj)])
            else:
                add("dve", (lambda e: e.tensor_scalar(out=r32[:, rj, :], in0=ps[:, bk, :], scalar1=0.0, scalar2=None,
                                                      op0=ALU.max)),
                    reads=[bank(bk)], writes=[("r32", rj)])
            add("dve", (lambda e: e.tensor_tensor(out=hT[:, fcl, :], in0=r32[:, rj, :], in1=r32[:, rj, :], op=ALU.mult)),
                reads=[("r32", rj)], writes=[HT(fcl)])

        zc = [0]

        def attn(b):
            par = b % 2
            tiles = []
            for hp in range(4):
                nk = 4 * b + 4
                for j, kt in enumerate(range(nk - 1, -1, -1)):
                    diag = kt - 4 * b
                    tiles.append(dict(hp=hp, kt=kt, k=j, first=(j == 0), last=(j == nk - 1), diag=diag,
                                      c0=(diag * 128 if diag > 0 else 0), n=len(tiles)))
            N = len(tiles)

            def QK(t, bk0, first_use):
                hp, kt, c0 = t["hp"], t["kt"], t["c0"]
                for hh in range(2):
                    bk = bk0 + hh
                    add("pe", (lambda e, hp=hp, kt=kt, hh=hh, bk=bk, c0=c0: e.matmul(
                        ps[:, bk, c0:512], lhsT=KT[hh * 64:(hh + 1) * 64, hp, kt * 128:(kt + 1) * 128],
                        rhs=qT[hh * 64:(hh + 1) * 64, par, hp, c0:512], start=True, stop=False, skip_group_check=True)),
                        reads=[("KT", kt // 4, hp), ("qT", par, hp)], writes=[bank(bk)])
                if t["diag"] >= 0:
                    for hh in range(2):
                        bk = bk0 + hh
                        add("pe", (lambda e, bk=bk, c0=c0: e.matmul(
                            ps[:, bk, c0:c0 + 128], lhsT=cbf[:, C_IDENT, :], rhs=cbf[:, C_MASK, :],
                            start=False, stop=False, skip_group_check=True)),
                            reads=[("cbf",)], writes=[bank(bk)])

            def S1(t):
                t["zs"] = 0
                QK(t, 0, True)

            def S2a(t):
                c0 = t["c0"]
                add("act", (lambda e, c0=c0: e.activation(
                    out=e1[:, :, c0:512], in_=ps[:, 0:2, c0:512], func=AF.Exp)),
                    reads=[bank(0), bank(1)], writes=[("e",)])

            def S2b(t):
                n, c0 = t["n"], t["c0"]
                sj = n % 3
                if c0 > 0:
                    add("dve", (lambda e, sj=sj, c0=c0: e.memset(sp_r[:, sj, :, 0:c0], 0.0)), writes=[("sp", sj)])
                add("act", (lambda e, sj=sj, c0=c0: e.activation(
                    out=sp_r[:, sj, :, c0:512], in_=e1[:, :, c0:512], func=AF.Ln, bias=1.0, scale=1.0)),
                    reads=[("e",)], writes=[("sp", sj)])

            def r_src(t):
                k, n = t["k"], t["n"]
                if k == 0:
                    return None, None
                if k == 1:
                    return sp_r[:, (n - 1) % 3], ("sp", (n - 1) % 3)
                return R_r[:, k % 3], ("R", k % 3)

            def S3(t):
                k, n = t["k"], t["n"]
                if t["last"] or k == 0:
                    return
                src, sres = r_src(t)
                dj = (k + 1) % 3
                add("dve", (lambda e, src=src, n=n, dj=dj: e.tensor_tensor(
                    out=R_r[:, dj], in0=src, in1=sp_r[:, n % 3], op=ALU.add)),
                    reads=[sres, ("sp", n % 3)], writes=[("R", dj)])

            def S4(t):
                n, c0 = t["n"], t["c0"]
                src, sres = r_src(t)
                QK(t, 2, False)
                for hh in range(2):
                    bk = 2 + hh
                    add("pe", (lambda e, n=n, hh=hh, bk=bk, c0=c0: e.matmul(
                        ps[:, bk, c0:512], lhsT=cbf[:, C_TRI, :], rhs=sp_r[:, n % 3, hh, c0:512],
                        start=False, stop=(src is None), skip_group_check=True)),
                        reads=[("sp", n % 3), ("cbf",)], writes=[bank(bk)])
                    if src is not None:
                        add("pe", (lambda e, hh=hh, bk=bk, c0=c0: e.matmul(
                            ps[:, bk, c0:512], lhsT=cbf[:, C_ONES, :], rhs=src[:, hh, c0:512],
                            start=False, stop=True, skip_group_check=True)),
                            reads=[sres, ("cbf",)], writes=[bank(bk)])

            def S5(t):
                n, c0 = t["n"], t["c0"]
                wj = n % 2
                add("act", (lambda e, wj=wj, c0=c0: e.activation(
                    out=W_r[:, wj, :, c0:512], in_=ps[:, 2:4, c0:512], func=AF.Exp)),
                    reads=[bank(2), bank(3)], writes=[("W", wj)])

            def S6(t):
                n, hp, kt, c0 = t["n"], t["hp"], t["kt"], t["c0"]
                wj = n % 2
                ob = 4 + hp % 2
                for hh in range(2):
                    h = 2 * hp + hh
                    add("pe", (lambda e, kt=kt, h=h, hh=hh, wj=wj, ob=ob, c0=c0, t=t: e.matmul(
                        ps[hh * 64:(hh + 1) * 64, ob, c0:512], lhsT=V[:, kt, h * 64:(h + 1) * 64],
                        rhs=W_r[:, wj, hh, c0:512], start=t["first"], stop=t["last"], skip_group_check=True)),
                        reads=[("V", kt), ("W", wj)], writes=[bank(ob)])
                if t["last"]:
                    epilogue(hp, ob)

            def epilogue(hp, ob):
                mb = 4 + (hp + 1) % 2
                rj = 0
                add("act", (lambda e: e.activation(out=sqb, in_=ps[:, ob, :], func=AF.Square)),
                    reads=[bank(ob)], writes=[("dT", 0)])
                add("pe", (lambda e: e.matmul(ps[:, mb, :], lhsT=cbf[:, C_BLK, :], rhs=sqb, start=True, stop=True)),
                    reads=[("dT", 0), ("cbf",)], writes=[bank(mb)])
                add("act", (lambda e: e.activation(out=rst[:, rj, :], in_=ps[:, mb, :], func=AF.Ln,
                                                   bias=RMS_EPS, scale=1.0)),
                    reads=[bank(mb)], writes=[("r32", 0)])
                add("act", (lambda e: e.activation(out=rst[:, rj, :], in_=rst[:, rj, :], func=AF.Exp, scale=-0.5)),
                    reads=[("r32", 0)], writes=[("r32", 0)])
                add("dve", (lambda e: e.scalar_tensor_tensor(
                    out=mixA[:, hp, :], in0=ps[:, ob, :], scalar=cols[:, hp:hp + 1], in1=rst[:, rj, :],
                    op0=ALU.mult, op1=ALU.mult)),
                    reads=[bank(ob), ("r32", 0), ("cols",)], writes=[("mixA", hp)])

            S1(tiles[0])
            for n in range(N + 2):
                if 1 <= n <= N:
                    S4(tiles[n - 1])
                if n < N:
                    S2a(tiles[n])
                if n + 1 < N:
                    S1(tiles[n + 1])
                if n < N:
                    S2b(tiles[n])
                    S3(tiles[n])
                if 1 <= n <= N:
                    S5(tiles[n - 1])
                if 2 <= n <= N + 1:
                    S6(tiles[n - 2])
                yield 1.0

        PRE_W, POST_W = 62.0, 275.0

        def drive(main_gen, nsteps, side_gens, side_total):
            ready = [0] * len(side_gens)
            alive = [True] * len(side_gens)
            done_w = 0.0
            rr = 0
            for s_ in range(nsteps):
                next(main_gen)
                budget = max(1.5, (side_total - done_w) / max(1, nsteps - s_))
                spent = 0.0
                while spent < budget:
                    cand = [i for i in range(len(side_gens)) if alive[i] and ready[i] <= s_]
                    if not cand:
                        break
                    i = cand[rr % len(cand)]
                    rr += 1
                    try:
                        y = next(side_gens[i])
                    except StopIteration:
                        alive[i] = False
                        continue
                    if isinstance(y, tuple):
                        ready[i] = s_ + y[1]
                    else:
                        spent += y
                done_w += spent
            for _ in main_gen:
                pass
            for g_ in side_gens:
                for _ in g_:
                    pass

        for _ in pre(0):
            pass
        for b in range(NBLK):
            sides, tot = [], 0.0
            if b >= 1:
                sides.append(post(b - 1))
                tot += POST_W
            if b + 1 < NBLK:
                sides.append(pre(b + 1))
                tot += PRE_W
            if b <= 2:
                tot *= 0.8
            nsteps = 16 * (b + 1) + 2
            drive(attn(b), nsteps, [itertools.chain(*sides)] if sides else [], tot)
        for _ in post(NBLK - 1):
            pass

        sch.finalize()
        finals = [(ent[0], ent[1]) for key, ent in sch.dma_sems.items()
                  if isinstance(key, tuple) and key[0] in ("xres_out", "dbg0", "dbg1", "dbg2")]
        with nc.Block() as block:
            @block.tensor
            def _(eng):
                sch.emit_engine("pe", eng)

            @block.scalar
            def _(eng):
                sch.emit_engine("act", eng)

            @block.vector
            def _(eng):
                sch.emit_engine("dve", eng)

            @block.gpsimd
            def _(eng):
                sch.emit_engine("pool", eng)

            @block.sync
            def _(eng):
                sch.emit_engine("sp", eng, final_waits=finals)
    return nc


def _consts():
    bf = ml_dtypes.bfloat16
    cb = np.zeros((NCB, 128, 128), np.float32)
    j = np.arange(128)[:, None]
    s = np.arange(128)[None, :]
    cb[C_IDENT] = np.eye(128)
    cb[C_TRI] = -(j >= s).astype(np.float32)
    cb[C_ONES] = -1.0
    blk = np.zeros((128, 128), np.float32)
    blk[:64, :64] = 1.0 / 64
    blk[64:, 64:] = 1.0 / 64
    cb[C_BLK] = blk
    for g, w in enumerate(POOL_WINDOWS):
        t = s
        diag = ((j <= t) & (j > t - w)).astype(np.float32) / w - (j == t)
        off = ((j - 128) > (t - w)).astype(np.float32) / w
        cnt = np.minimum(t + 1, w).astype(np.float32)
        first = ((j <= t) & (j > t - w)).astype(np.float32) / cnt - (j == t)
        cb[C_POOL + 3 * g + 0] = diag
        cb[C_POOL + 3 * g + 1] = off
        cb[C_POOL + 3 * g + 2] = first
    cb[C_MASK] = np.where(j >= s, NEG, 0.0)
    cbf = np.ascontiguousarray(cb.transpose(1, 0, 2)).astype(bf)
    return cbf, np.eye(128, dtype=np.float32)


_NC_CACHE = {}


def _get_nc(S):
    if S not in _NC_CACHE:
        _NC_CACHE[S] = build_nc(S)
    return _NC_CACHE[S]


def make_in_maps(x, p, emb_ln_g, emb_ln_b, w_in, attn_out_g, w_pool, pool_scale, w_out,
                 ln1_g, ln1_b, w_up, w_down, ln2_g, ln2_b, w_ple, w_ple_gate, ln3_g, ln3_b):
    f = lambda a: np.ascontiguousarray(np.asarray(a, dtype=np.float32))
    B = x.shape[0]
    cbf, identf = _consts()
    lnp = np.stack([f(emb_ln_g), f(emb_ln_b), f(ln1_g)[0], f(ln1_b)[0], f(ln2_g)[0], f(ln2_b)[0],
                    f(ln3_g)[0], f(ln3_b)[0]], axis=0)
    cols = np.concatenate([f(attn_out_g)[0].reshape(4, 128).T, f(pool_scale)[0].reshape(4, 128).T], axis=1)
    shared = {
        "w_in": f(w_in)[0], "w_out": f(w_out)[0], "w_up": f(w_up)[0], "w_down": f(w_down)[0],
        "w_ple": f(w_ple)[0], "w_gate": f(w_ple_gate)[0], "w_pool": f(w_pool)[0],
        "lnp": np.ascontiguousarray(lnp), "cols": np.ascontiguousarray(cols),
        "cbf": cbf, "identf": identf,
    }
    xs = f(x)
    ps_ = f(p)[0]
    in_maps = []
    for c in range(B):
        m = dict(shared)
        m["x"] = xs[c]
        m["p"] = ps_[c]
        in_maps.append(m)
    return in_maps


def kernel(**inputs):
    x = inputs["x"]
    B, S, _ = x.shape
    nc = _get_nc(S)
    in_maps = make_in_maps(**inputs)
    res = run_bass_kernel_spmd(nc, in_maps, core_ids=list(range(B)))
    out = np.stack([np.asarray(r["out"], dtype=np.float32) for r in res.results], axis=0)
    return out
```
